# Optimizing a Trainium2 kernel written in Bass

```python
import jax, jax.numpy as jnp
from jax import lax
import numpy as np

D_MODEL = 1024
BATCH = 8
SEQ = 2048
DEPTH = 4
DEC_BATCH = 128
DEC_SEQ = 4
PAST_LEN = 16384
PAGE_SIZE = 128

D_CONV = D_MODEL // 2
CONV_WIDTH = 31
D_POOL = D_MODEL // 2
POOL_WINDOWS = (2, 4, 8, 16)
POOL_GROUPS = len(POOL_WINDOWS)
POOL_GC = D_POOL // POOL_GROUPS
POOL_BUF = max(POOL_WINDOWS) - 1
D_SGU = D_MODEL // 2
SGU_GROUPS = 4
SGU_GC = D_SGU // SGU_GROUPS
CHUNK = 128
N_BRANCH = 3
D_FF = 2816
D_IN = 2 * D_CONV + D_POOL + 2 * D_SGU + N_BRANCH * D_MODEL
IN_SPLITS = (2 * D_CONV, 2 * D_CONV + D_POOL, 2 * D_CONV + D_POOL + D_SGU, 2 * D_CONV + D_POOL + 2 * D_SGU)
RMS_EPS = 1e-6
LN_EPS = 1e-5

kernel_name = "hybrid_conv_pool_sgu_decoder_step"


def _rmsnorm(x, g):
    x32 = x.astype(jnp.float32)
    y = x32 * lax.rsqrt(jnp.mean(x32 * x32, axis=-1, keepdims=True) + RMS_EPS)
    return (y * g.astype(jnp.float32)).astype(x.dtype)


def _layernorm(x, g, b):
    x32 = x.astype(jnp.float32)
    mu = jnp.mean(x32, axis=-1, keepdims=True)
    var = jnp.mean(jnp.square(x32 - mu), axis=-1, keepdims=True)
    y = (x32 - mu) * lax.rsqrt(var + LN_EPS) * g.astype(jnp.float32) + b.astype(jnp.float32)
    return y.astype(x.dtype)


def _swiglu_ffn(x, g, w_gu, w_d):
    h = _rmsnorm(x, g)
    gate, up = jnp.split(h @ w_gu, 2, axis=-1)
    return (jax.nn.silu(gate) * up) @ w_d


def _conv_module(a2, buf, dw_w, dw_b, ln_g, ln_b, w_out):
    a_val, a_gate = jnp.split(a2, 2, axis=-1)
    a = a_val * jax.nn.sigmoid(a_gate)
    padded = jnp.concatenate([buf.astype(a.dtype), a], axis=1)
    new_buf = padded[:, -(CONV_WIDTH - 1):]
    y = lax.conv_general_dilated(
        padded, dw_w[:, None, :].astype(a.dtype), window_strides=(1,), padding='VALID',
        dimension_numbers=('NWC', 'WIO', 'NWC'), feature_group_count=D_CONV)
    y = _layernorm(y + dw_b, ln_g, ln_b)
    return jax.nn.silu(y) @ w_out, new_buf


def _pool_mixer(p, buf, start, pool_w, pool_scale, w_out):
    bsz, L, _ = p.shape
    padded = jnp.concatenate([buf.astype(p.dtype), p], axis=1)
    new_buf = padded[:, -POOL_BUF:]
    cs = jnp.cumsum(padded.astype(jnp.float32), axis=1)
    cs = jnp.concatenate([jnp.zeros((bsz, 1, D_POOL), jnp.float32), cs], axis=1)
    pos = start + jnp.arange(L, dtype=jnp.int32)
    hi = cs[:, POOL_BUF + 1:POOL_BUF + 1 + L]
    means = []
    for g, w in enumerate(POOL_WINDOWS):
        sl = slice(g * POOL_GC, (g + 1) * POOL_GC)
        lo = cs[:, POOL_BUF + 1 - w:POOL_BUF + 1 - w + L, sl]
        cnt = jnp.minimum(pos + 1, w).astype(jnp.float32)[None, :, None]
        means.append((hi[..., sl] - lo) / cnt)
    m = (jnp.concatenate(means, axis=-1) - p.astype(jnp.float32)).astype(p.dtype)
    m = m.reshape(bsz, L, POOL_GROUPS, POOL_GC)
    y = jnp.einsum('blgc,gcd->blgd', m, pool_w).reshape(bsz, L, D_POOL)
    return (y * pool_scale) @ w_out, new_buf


def _sgu_mixer(u, v, ln_g, ln_b, w_s, b_s, w_out):
    bsz, L, _ = v.shape
    vn = _layernorm(v, ln_g, ln_b)
    n_chunks = -(-L // CHUNK)
    pad = n_chunks * CHUNK - L
    vp = jnp.pad(vn, ((0, 0), (0, pad), (0, 0))).reshape(bsz, n_chunks, CHUNK, SGU_GROUPS, SGU_GC)
    w_causal = jnp.tril(w_s)
    mixed = jnp.einsum('gts,bnsgc->bntgc', w_causal, vp) + jnp.transpose(b_s)[None, None, :, :, None]
    mixed = mixed.reshape(bsz, n_chunks * CHUNK, D_SGU)[:, :L]
    return (u * mixed) @ w_out, vn


def _layer(x, conv_buf, pool_buf, start, ffn1_norm, ffn1_w_gate_up, ffn1_w_down, mix_norm, w_in,
           conv_dw_w, conv_dw_b, conv_ln_g, conv_ln_b, w_conv_out, pool_w, pool_scale, w_pool_out,
           sgu_ln_g, sgu_ln_b, sgu_w, sgu_b, w_sgu_out, w_o, ffn2_norm, ffn2_w_gate_up, ffn2_w_down):
    x = x + 0.5 * _swiglu_ffn(x, ffn1_norm, ffn1_w_gate_up, ffn1_w_down)
    h = _rmsnorm(x, mix_norm)
    z = h @ w_in
    a2, p, u, v, gl = jnp.split(z, IN_SPLITS, axis=-1)
    y_a, conv_buf = _conv_module(a2, conv_buf, conv_dw_w, conv_dw_b, conv_ln_g, conv_ln_b, w_conv_out)
    y_b, pool_buf = _pool_mixer(p, pool_buf, start, pool_w, pool_scale, w_pool_out)
    y_c, vn = _sgu_mixer(u, v, sgu_ln_g, sgu_ln_b, sgu_w, sgu_b, w_sgu_out)
    g_a, g_b, g_c = jnp.split(jax.nn.sigmoid(gl), N_BRANCH, axis=-1)
    x = x + (g_a * y_a + g_b * y_b + g_c * y_c) @ w_o
    x = x + 0.5 * _swiglu_ffn(x, ffn2_norm, ffn2_w_gate_up, ffn2_w_down)
    return x, conv_buf, pool_buf, vn


def _trunk(x, conv_state, pool_state, start, layer_params, final_norm):
    conv_out, pool_out, v_out = [], [], []
    for l in range(DEPTH):
        x, cb, pb, vn = _layer(x, conv_state[l], pool_state[l], start, *[prm[l] for prm in layer_params])
        conv_out.append(cb)
        pool_out.append(pb)
        v_out.append(vn)
    return _rmsnorm(x, final_norm), jnp.stack(conv_out), jnp.stack(pool_out), jnp.stack(v_out)


def setup_inputs(seed: int = 0) -> dict:
    key = jax.random.key(seed)
    ks = iter(jax.random.split(key, 40))

    def nrm(shape, scale):
        return jax.random.normal(next(ks), shape, jnp.float32) * scale

    return {
        "x_prompt": nrm((BATCH, SEQ, D_MODEL), 1.0),
        "x_sample": nrm((DEC_BATCH, DEC_SEQ, D_MODEL), 1.0),
        "state_conv": nrm((DEPTH, DEC_BATCH, CONV_WIDTH - 1, D_CONV), 0.5),
        "state_pool": nrm((DEPTH, DEC_BATCH, POOL_BUF, D_POOL), 0.6),
        "ffn1_norm": 1.0 + nrm((DEPTH, D_MODEL), 0.05),
        "ffn1_w_gate_up": nrm((DEPTH, D_MODEL, 2 * D_FF), D_MODEL ** -0.5),
        "ffn1_w_down": nrm((DEPTH, D_FF, D_MODEL), D_FF ** -0.5),
        "mix_norm": 1.0 + nrm((DEPTH, D_MODEL), 0.05),
        "w_in": nrm((DEPTH, D_MODEL, D_IN), D_MODEL ** -0.5),
        "conv_dw_w": nrm((DEPTH, CONV_WIDTH, D_CONV), CONV_WIDTH ** -0.5),
        "conv_dw_b": nrm((DEPTH, D_CONV), 0.02),
        "conv_ln_g": 1.0 + nrm((DEPTH, D_CONV), 0.05),
        "conv_ln_b": nrm((DEPTH, D_CONV), 0.02),
        "w_conv_out": nrm((DEPTH, D_CONV, D_MODEL), D_CONV ** -0.5),
        "pool_w": nrm((DEPTH, POOL_GROUPS, POOL_GC, POOL_GC), POOL_GC ** -0.5),
        "pool_scale": 1.0 + nrm((DEPTH, D_POOL), 0.1),
        "w_pool_out": nrm((DEPTH, D_POOL, D_MODEL), D_POOL ** -0.5),
        "sgu_ln_g": 1.0 + nrm((DEPTH, D_SGU), 0.05),
        "sgu_ln_b": nrm((DEPTH, D_SGU), 0.02),
        "sgu_w": nrm((DEPTH, SGU_GROUPS, CHUNK, CHUNK), 0.5 * CHUNK ** -0.5),
        "sgu_b": 1.0 + nrm((DEPTH, SGU_GROUPS, CHUNK), 0.05),
        "w_sgu_out": nrm((DEPTH, D_SGU, D_MODEL), D_SGU ** -0.5),
        "w_o": nrm((DEPTH, D_MODEL, D_MODEL), D_MODEL ** -0.5),
        "ffn2_norm": 1.0 + nrm((DEPTH, D_MODEL), 0.05),
        "ffn2_w_gate_up": nrm((DEPTH, D_MODEL, 2 * D_FF), D_MODEL ** -0.5),
        "ffn2_w_down": nrm((DEPTH, D_FF, D_MODEL), D_FF ** -0.5),
        "final_norm": 1.0 + nrm((D_MODEL,), 0.05),
    }


def reference(x_prompt, x_sample, state_conv, state_pool, ffn1_norm, ffn1_w_gate_up, ffn1_w_down, mix_norm, w_in,
              conv_dw_w, conv_dw_b, conv_ln_g, conv_ln_b, w_conv_out, pool_w, pool_scale, w_pool_out,
              sgu_ln_g, sgu_ln_b, sgu_w, sgu_b, w_sgu_out, w_o, ffn2_norm, ffn2_w_gate_up, ffn2_w_down, final_norm):
    layer_params = (ffn1_norm, ffn1_w_gate_up, ffn1_w_down, mix_norm, w_in,
                    conv_dw_w, conv_dw_b, conv_ln_g, conv_ln_b, w_conv_out, pool_w, pool_scale, w_pool_out,
                    sgu_ln_g, sgu_ln_b, sgu_w, sgu_b, w_sgu_out, w_o, ffn2_norm, ffn2_w_gate_up, ffn2_w_down)
    bsz = x_prompt.shape[0]
    zero_conv = jnp.zeros((DEPTH, bsz, CONV_WIDTH - 1, D_CONV), x_prompt.dtype)
    zero_pool = jnp.zeros((DEPTH, bsz, POOL_BUF, D_POOL), x_prompt.dtype)
    y_prompt, conv_prompt, pool_prompt, _ = _trunk(x_prompt, zero_conv, zero_pool, 0, layer_params, final_norm)
    y_sample, conv_sample, pool_sample, chunk_v_sample = _trunk(x_sample, state_conv, state_pool, PAST_LEN,
                                                                layer_params, final_norm)
    return (y_prompt, y_sample, conv_prompt, conv_sample, pool_prompt, pool_sample, chunk_v_sample)
```

```python
import numpy as np
from contextlib import ExitStack
import concourse.bass as bass
import concourse.mybir as mybir
from concourse.bass_utils import run_bass_kernel_spmd

F32 = mybir.dt.float32
BF16 = mybir.dt.bfloat16
ALU = mybir.AluOpType
AF = mybir.ActivationFunctionType

import os
DBG_V = int(os.environ.get("DBG_V", "0"))
NCORES = 8
DEPTH = 4
D = 1024
DFF = 2816
DIN = 5632
TP = 2048
TS = 64
T = TP + TS
NSEQ = 16
TILES = [(0, 512), (512, 512), (1024, 512), (1536, 512), (2048, 64)]
HALVES = [[0, 1], [2, 3, 4]]
HBASE = [0, 1024]
HLEN = [1024, 1088]
CW = 31
PB = 15
WIN = (2, 4, 8, 16)
RMS_EPS = 1e-6
LN_EPS = 1e-5
SLAB = 2048
NS = 2
NB = 3
LA = 2
HG = [(0, 11), (11, 11)]


class _Rec:
    def __init__(self):
        self.call = None

    def __getattr__(self, name):
        def f(*a, **kw):
            self.call = (name, a, kw)
            return None
        return f


class Prog:
    ENG = ("pe", "act", "dve", "pool", "sp")

    def __init__(self, nc, stack):
        self.nc = nc
        self.stack = stack
        self.q = {e: [] for e in self.ENG}
        self.sem = {e: stack.enter_context(nc.semaphore("s_" + e)) for e in self.ENG}
        self.cnt = {e: 0 for e in self.ENG}
        self.base = []
        self.dma_sems = []

    def barrier(self):
        b = [(self.sem[e], self.cnt[e]) for e in ("pe", "act", "dve") if self.cnt[e] > 0]
        b += [(s[0], s[1]) for s in self.dma_sems if s[1] > 0]
        self.base = b

    def op(self, eng, fn, waits=(), inc=True, nobar=False):
        ev = None
        if inc:
            self.cnt[eng] += 1
            ev = (self.sem[eng], self.cnt[eng])
        w = flat(waits)
        if not nobar:
            w = w + self.base
        rec = _Rec()
        fn(rec)
        name, a, kw = rec.call
        self.q[eng].append(((lambda e, name=name, a=a, kw=kw: getattr(e, name)(*a, **kw)), w, ev, 1))
        return ev

    def newsem(self, name):
        return [self.stack.enter_context(self.nc.semaphore(name)), 0]

    def dma(self, eng, out, in_, sem, waits=(), nobar=False, **kw):
        sem[1] += 16
        ev = (sem[0], sem[1])
        w = flat(waits)
        if not nobar:
            w = w + self.base
        self.q[eng].append((lambda e: e.dma_start(out=out, in_=in_, **kw), w, ev, 16))
        return ev

    def emit(self, block):
        engs = {"pe": block.tensor, "act": block.scalar, "dve": block.vector,
                "pool": block.gpsimd, "sp": block.sync}
        for ename, deco in engs.items():
            ops = self.q[ename]

            def body(e, ops=ops):
                seen = {}
                for fn, waits, ev, incv in ops:
                    for (s, v) in waits:
                        if seen.get(s.name, 0) >= v:
                            continue
                        seen[s.name] = v
                        e.wait_ge(s, v)
                    ins = fn(e)
                    if ev is not None:
                        ins.then_inc(ev[0], incv)
            deco(body)


def flat(w):
    out = []
    if w is None:
        return out
    if isinstance(w, tuple) and len(w) == 2 and not isinstance(w[0], (tuple, list)) and w[0] is not None and not isinstance(w[1], (tuple, list)):
        return [w]
    for x in w:
        out.extend(flat(x))
    return out


class Ring:
    def __init__(self, bufs):
        self.bufs = bufs
        self.free = [None] * len(bufs)
        self.nxt = 0

    def get(self):
        i = self.nxt
        self.nxt = (i + 1) % len(self.bufs)
        return i, self.bufs[i], self.free[i]

    def release(self, i, evs):
        self.free[i] = evs


def build_program(depth=DEPTH, limit=None):
    nc = bass.Bass("TRN2", target_bir_lowering=False)

    def din(name, shape):
        return nc.dram_tensor(name, list(shape), F32, kind="ExternalInput").ap()

    def dout(name, shape):
        return nc.dram_tensor(name, list(shape), F32, kind="ExternalOutput").ap()

    xin = din("xin", [T, D])
    sconv = din("sconv", [DEPTH, NSEQ, 30, 512])
    spool = din("spool", [DEPTH, NSEQ, 15, 512])
    W = {}
    for name, shape in [
        ("ffn1_norm", [DEPTH, D]), ("ffn1_w_gate_up", [DEPTH, D, 2 * DFF]), ("ffn1_w_down", [DEPTH, DFF, D]),
        ("mix_norm", [DEPTH, D]), ("w_in", [DEPTH, D, DIN]), ("conv_dw_w", [DEPTH, CW, 512]),
        ("conv_dw_b", [DEPTH, 512]), ("conv_ln_g", [DEPTH, 512]), ("conv_ln_b", [DEPTH, 512]),
        ("w_conv_out", [DEPTH, 512, D]), ("pool_w", [DEPTH, 4, 128, 128]), ("pool_scale", [DEPTH, 512]),
        ("w_pool_out", [DEPTH, 512, D]), ("sgu_ln_g", [DEPTH, 512]), ("sgu_ln_b", [DEPTH, 512]),
        ("sgu_w", [DEPTH, 4, 128, 128]), ("sgu_b", [DEPTH, 4, 128]), ("w_sgu_out", [DEPTH, 512, D]),
        ("w_o", [DEPTH, D, D]), ("ffn2_norm", [DEPTH, D]), ("ffn2_w_gate_up", [DEPTH, D, 2 * DFF]),
        ("ffn2_w_down", [DEPTH, DFF, D]), ("final_norm", [D]),
    ]:
        W[name] = din(name, shape)
    y_out = dout("y", [T, D])
    conv_p = dout("conv_p", [DEPTH, 30, 512])
    conv_s = dout("conv_s", [DEPTH, NSEQ, 30, 512])
    pool_p = dout("pool_p", [DEPTH, 15, 512])
    pool_s = dout("pool_s", [DEPTH, NSEQ, 15, 512])
    cv_s = dout("cv_s", [DEPTH, TS, 512])

    st = ExitStack()
    with st:
        def sb(name, shape, dt):
            return st.enter_context(nc.sbuf_tensor(name, list(shape), dt))

        xT = sb("xT", [128, 8, T], F32)
        hT = sb("hT", [128, 8, T], BF16)
        stg = sb("stg", [128, NS, SLAB], F32)
        wbf = sb("wbf", [128, NB, SLAB], BF16)
        ident_f = sb("ident_f", [128, 128], F32)
        ident_b = sb("ident_b", [128, 128], BF16)
        ones_b = sb("ones_b", [128, 128], BF16)
        ones_f = sb("ones_f", [128, 128], F32)
        neghalf = sb("neghalf", [128, 512], F32)
        rc = sb("rc", [128, 16], F32)
        epsc = sb("epsc", [128, 2], F32)
        gn = sb("gn", [128, 13, 8], F32)
        cw = sb("cw", [128, DEPTH, 4, CW], F32)
        cb = sb("cb", [128, DEPTH, 4], F32)
        clg = sb("clg", [128, DEPTH, 4], F32)
        clb = sb("clb", [128, DEPTH, 4], F32)
        psc = sb("psc", [128, DEPTH, 4], F32)
        slg = sb("slg", [128, DEPTH, 4], F32)
        slb = sb("slb", [128, DEPTH, 4], F32)
        lstage = sb("lstage", [128, 4, 128], F32)
        pw_b = sb("pw_b", [128, 4, 128], BF16)
        wct = sb("wct", [128, 4, 128], BF16)
        bds = sb("bds", [64, 4, 64], F32)
        bd_b = sb("bd_b", [64, 4, 64], BF16)
        bs = sb("bs", [1, 4, 192], F32)
        ahalo = sb("ahalo", [128, 4, 30], BF16)
        phalo = sb("phalo", [128, 4, 16], F32)
        NSCR = 29952
        scr = sb("scr", [128, NSCR], BF16)
        banks = [st.enter_context(nc.psum_tensor("bk%d" % i, [128, 512], F32)) for i in range(8)]

        P = Prog(nc, st)
        block = st.enter_context(nc.Block())
        bk = Ring(banks)

        def sv(off, shape, dt):
            n = int(np.prod(shape))
            if dt == F32:
                assert off % 2 == 0
                v = scr[:, off:off + 2 * n].bitcast(F32)
                end = off + 2 * n
            else:
                v = scr[:, off:off + n]
                end = off + n
            assert end <= NSCR, (off, shape, end)
            if len(shape) == 1:
                return v, end
            names = " ".join("d%d" % i for i in range(len(shape)))
            kw = {"d%d" % i: shape[i] for i in range(len(shape) - 1)}
            return v.rearrange("p (%s) -> p %s" % (names, names), **kw), end

        def bcast_col(col, n):
            return bass.AP(col.tensor, col.offset, [list(col.ap[0]), [0, n]])

        s_stage = [P.newsem("stage%d" % i) for i in range(NS)]
        s_in2 = [P.newsem("s_in%d" % i) for i in range(2)]
        s_par = P.newsem("s_par")
        s_lstage = P.newsem("s_lstage")
        s_bds = P.newsem("s_bds")
        s_bs = P.newsem("s_bs")
        s_out2 = [P.newsem("s_out%d" % i) for i in range(2)]
        s_outv = P.newsem("s_outv")
        s_sti2 = [P.newsem("s_sti%d" % i) for i in range(2)]
        P.dma_sems = s_in2 + [s_lstage, s_bds, s_bs, s_outv] + s_out2 + s_sti2

        e = P.op("pool", lambda e_: e_.memset(ident_f[:], 1.0))
        e_ident = P.op("pool", lambda e_: e_.affine_select(out=ident_f[:], in_=ident_f[:], pattern=[[-1, 128]],
                                                            compare_op=ALU.is_equal, fill=0.0, base=0, channel_multiplier=1), [e])
        e_identb = P.op("dve", lambda e_: e_.tensor_copy(out=ident_b[:], in_=ident_f[:]), [e_ident])
        e_onesb = P.op("dve", lambda e_: e_.memset(ones_b[:], 1.0))
        e_onesf = P.op("dve", lambda e_: e_.memset(ones_f[:], 1.0))
        e_nh = P.op("pool", lambda e_: e_.memset(neghalf[:], -0.5))
        for t_ in range(16):
            e_rc = P.op("dve", (lambda t_: lambda e_: e_.memset(rc[:, t_:t_ + 1], 1.0 / (t_ + 1)))(t_))
        e_eps0 = P.op("dve", lambda e_: e_.memset(epsc[:, 0:1], float(D * RMS_EPS)))
        e_eps1 = P.op("dve", lambda e_: e_.memset(epsc[:, 1:2], float(LN_EPS)))
        e_const = [e_ident, e_identb, e_onesb, e_onesf, e_nh, e_rc, e_eps0, e_eps1]

        _sp_n0 = len(P.q["sp"])
        gn4 = gn[:, 0:12, :].rearrange("p (l j) k -> p l j k", j=3)
        pe_ = []
        for j, nm in enumerate(["ffn1_norm", "mix_norm", "ffn2_norm"]):
            for l in range(DEPTH):
                pe_.append(P.dma("sp", gn[:, 3 * l + j, :], W[nm][l].rearrange("(k p) -> p k", p=128), s_par, allow_slow_non_contiguous=True))
        pe_.append(P.dma("sp", gn[:, 12, :], W["final_norm"].rearrange("(k p) -> p k", p=128), s_par, allow_slow_non_contiguous=True))
        for l in range(DEPTH):
            for c_ in range(4):
                pe_.append(P.dma("sp", cw[:, l, c_, :], W["conv_dw_w"][l][:, c_ * 128:(c_ + 1) * 128].rearrange("j p -> p j"), s_par, allow_slow_non_contiguous=True))
        for tile_, nm in [(cb, "conv_dw_b"), (clg, "conv_ln_g"), (clb, "conv_ln_b"), (psc, "pool_scale"), (slg, "sgu_ln_g"), (slb, "sgu_ln_b")]:
            pe_.append(P.dma("sp", tile_[:], W[nm].rearrange("l (c p) -> p l c", p=128), s_par, allow_slow_non_contiguous=True))
        e_par = pe_[-1]
        e_gn = P.op("dve", lambda e_: e_.tensor_scalar(out=gn[:], in0=gn[:], scalar1=32.0, scalar2=None, op0=ALU.mult), [e_par])
        e_cw = P.op("dve", lambda e_: e_.tensor_scalar(out=cw[:], in0=cw[:], scalar1=0.5, scalar2=None, op0=ALU.mult), [e_par])
        e_params = [e_par, e_gn, e_cw]
        _sp_n1 = len(P.q["sp"])

        x_ev = [None] * 5
        h_ev = [None] * 5
        h_rd = [None] * 5
        out_evs = []

        steps = []

        def add_step(slabs, compute, name=None):
            if name is not None:
                compute.__name__ = name
            steps.append((slabs, compute))
        marks = []

        def load_x():
            xs0, o = sv(0, [1024], F32)
            xs1, o = sv(o, [1024], F32)
            xring = Ring([xs0, xs1])
            last = {}
            for s_ in range(17):
                tok0 = s_ * 128
                n = 128 if s_ < 16 else 64
                tt = min(s_ // 4, 4)
                i, xs, fr = xring.get()
                ed = P.dma("sp", xs[0:n, :], xin[tok0:tok0 + n, :], s_in2[i], [fr])
                evs = []
                for hf in range(2):
                    bi, b, bfr = bk.get()
                    for kk in range(4):
                        k = hf * 4 + kk
                        ep = P.op("pe", (lambda b=b, kk=kk, xs=xs, k=k, n=n: lambda e_: e_.matmul(
                            b[:, kk * 128:kk * 128 + n], lhsT=xs[0:n, k * 128:(k + 1) * 128], rhs=ident_f[0:n, 0:n],
                            start=True, stop=True))(), [ed, bfr, e_const], inc=(kk == 3))
                    src = b[:, :].rearrange("p (a c) -> p a c", a=4)[:, :, 0:n]
                    ec = P.op("act", (lambda src=src, hf=hf, tok0=tok0, n=n: lambda e_: e_.activation(
                        out=xT[:, hf * 4:(hf + 1) * 4, tok0:tok0 + n], in_=src, func=AF.Copy))(), [ep])
                    bk.release(bi, ec)
                    evs.append(ep)
                    x_ev[tt] = ec
                xring.release(i, evs[-1])

        I32 = mybir.dt.int32

        def rsqrt_nr(xs, ys, n, waits):
            xi = xs[:, 0:n].bitcast(I32)
            yi = ys[:, 0:n].bitcast(I32)
            e = P.op("dve", lambda e_: e_.tensor_scalar(out=yi, in0=xi, scalar1=1, scalar2=None, op0=ALU.arith_shift_right), waits)
            e = P.op("dve", lambda e_: e_.tensor_scalar(out=yi, in0=yi, scalar1=-1, scalar2=0x5f3759df, op0=ALU.mult, op1=ALU.add), [e])
            ti, tb, tf = bk.get()
            for it in range(2):
                e = P.op("dve", lambda e_: e_.tensor_tensor(out=tb[:, 0:n], in0=ys[:, 0:n], in1=ys[:, 0:n], op=ALU.mult), [e, tf])
                e = P.op("dve", lambda e_: e_.tensor_tensor(out=tb[:, 0:n], in0=tb[:, 0:n], in1=xs[:, 0:n], op=ALU.mult), [e])
                e = P.op("dve", lambda e_: e_.tensor_scalar(out=tb[:, 0:n], in0=tb[:, 0:n], scalar1=-0.5, scalar2=1.5, op0=ALU.mult, op1=ALU.add), [e])
                e = P.op("dve", lambda e_: e_.tensor_tensor(out=ys[:, 0:n], in0=ys[:, 0:n], in1=tb[:, 0:n], op=ALU.mult), [e])
            bk.release(ti, e)
            return e

        def rsqrt_act(src, ys, n, bias_col, waits):
            e = P.op("act", lambda e_: e_.activation(out=ys[:, 0:n], in_=src, func=AF.Ln, bias=epsc[:, bias_col:bias_col + 1]), [waits, e_const])
            e = P.op("act", lambda e_: e_.activation(out=ys[:, 0:n], in_=ys[:, 0:n], func=AF.Exp, scale=-0.5), [e])
            return e

        def rms_stats(tt, sqv, rsr):
            t0, n = TILES[tt]
            bi, b, bfr = bk.get()
            for hf in range(2):
                ea = P.op("act", (lambda hf=hf: lambda e_: e_.activation(
                    out=sqv[:, :, 0:n], in_=xT[:, hf * 4:(hf + 1) * 4, t0:t0 + n], func=AF.Square))(),
                    [x_ev[tt], sq_free[0]])
                for kk in range(4):
                    ep = P.op("pe", (lambda kk=kk, hf=hf: lambda e_: e_.matmul(
                        b[:, 0:n], lhsT=ones_b[:], rhs=sqv[:, kk, 0:n], start=(hf == 0 and kk == 0),
                        stop=(hf == 1 and kk == 3)))(), [ea, bfr, e_const], inc=(kk == 3))
                sq_free[0] = ep
            ri, rs, rfr = rsr.get()
            xs_, ys_ = rs
            eo = rsqrt_act(b[:, 0:n], ys_, n, 0, [ep, rfr])
            bk.release(bi, eo)
            rs = ys_
            return rs, ri, eo

        sq_free = [None]

        def rmsnorm(gidx, tiles, sq_off):
            sqv, o = sv(sq_off, [4, 512], BF16)
            r1, o = sv(o, [512], F32)
            rsr = Ring([(None, r1)])
            for tt in tiles:
                t0, n = TILES[tt]
                rs, ri, eo = rms_stats(tt, sqv, rsr)
                for k in range(8):
                    ed = P.op("dve", (lambda k=k: lambda e_: e_.scalar_tensor_tensor(
                        out=hT[:, k, t0:t0 + n], in0=xT[:, k, t0:t0 + n], scalar=gn[:, gidx, k:k + 1],
                        in1=rs[:, 0:n], op0=ALU.mult, op1=ALU.mult))(), [eo, h_rd[tt], e_params])
                rsr.release(ri, ed)
                h_ev[tt] = ed
            return o

        def ffn(l, which):
            wgu = W["ffn%d_w_gate_up" % which][l]
            wd = W["ffn%d_w_down" % which][l]
            gidx = l * 3 + (0 if which == 1 else 2)
            hid, o = sv(0, [11, T], BF16)
            sg0, o2 = sv(o, [512], F32)
            sg1, o2 = sv(o2, [512], F32)
            sgr = Ring([sg0, sg1])
            hid_ev = [None] * 5

            def norm_step(sl, ev):
                if which == 1 and l > 0:
                    P.base = []
                else:
                    P.barrier()
                rmsnorm(gidx, range(5) if which == 1 else [2, 3, 4], o2)
            add_step([], norm_step)

            def up_step(j):
                def f(sl, ev):
                    gv = sl[0][:, 0:1024].rearrange("p (k c) -> p k c", k=8)
                    uv = sl[0][:, 1024:2048].rearrange("p (k c) -> p k c", k=8)
                    last = None
                    for tt in range(5):
                        t0, n = TILES[tt]
                        ia, ba, fa = bk.get()
                        ib, bb, fb = bk.get()
                        for k in range(8):
                            epa = P.op("pe", (lambda k=k: lambda e_: e_.matmul(ba[:, 0:n], lhsT=gv[:, k, :], rhs=hT[:, k, t0:t0 + n],
                                                                                start=(k == 0), stop=(k == 7)))(), [ev[0], h_ev[tt], fa], inc=(k == 7))
                        for k in range(8):
                            epb = P.op("pe", (lambda k=k: lambda e_: e_.matmul(bb[:, 0:n], lhsT=uv[:, k, :], rhs=hT[:, k, t0:t0 + n],
                                                                                start=(k == 0), stop=(k == 7)))(), [fb], inc=(k == 7))
                        h_rd[tt] = epb
                        si, sgv, sfr = sgr.get()
                        ea = P.op("act", lambda e_: e_.activation(out=sgv[:, 0:n], in_=ba[:, 0:n], func=AF.Silu), [epa, sfr])
                        bk.release(ia, ea)
                        ed = P.op("dve", lambda e_: e_.tensor_tensor(out=hid[:, j, t0:t0 + n], in0=bb[:, 0:n], in1=sgv[:, 0:n], op=ALU.mult), [ea, epb])
                        bk.release(ib, ed)
                        sgr.release(si, ed)
                        hid_ev[tt] = ed
                        last = epb
                    return last
                return f

            def down_step(nk, m):
                def f(sl, ev):
                    wv = sl[0][:, 0:nk * 128].rearrange("p (k c) -> p k c", k=nk)
                    last = None
                    for tt in range(5):
                        t0, n = TILES[tt]
                        ib, b, fb = bk.get()
                        for k in range(nk):
                            ep = P.op("pe", (lambda k=k: lambda e_: e_.matmul(b[:, 0:n], lhsT=wv[:, k, :], rhs=hid[:, k, t0:t0 + n],
                                                                               start=(k == 0), stop=(k == nk - 1)))(), [ev[0], hid_ev[tt], fb], inc=(k == nk - 1))
                        ed = P.op("dve", lambda e_: e_.scalar_tensor_tensor(out=xT[:, m, t0:t0 + n], in0=b[:, 0:n], scalar=0.5,
                                                                            in1=xT[:, m, t0:t0 + n], op0=ALU.mult, op1=ALU.add), [ep])
                        bk.release(ib, ed)
                        x_ev[tt] = ed
                        last = ep
                    return last
                return f

            for (c0, ncn) in HG:
                for j in range(ncn):
                    c = c0 + j
                    add_step([[(0, 8, 128, wgu[:, c * 128:(c + 1) * 128]),
                               (1024, 8, 128, wgu[:, DFF + c * 128:DFF + (c + 1) * 128])]], up_step(j), "ffn_up")
                for m in range(8):
                    add_step([[(0, ncn, 128, wd[c0 * 128:(c0 + ncn) * 128, m * 128:(m + 1) * 128])]], down_step(ncn, m), "ffn_down")

        def ln_fm(buf, col0, n, gt, bt, l, func, sqv, ver, w0, extra_out=None):
            ea = P.op("act", lambda e_: e_.activation(out=sqv[:, :, 0:n], in_=buf[:, :, col0:col0 + n], func=AF.Square), [sq_free[0], w0])
            i1, b1, f1 = bk.get()
            i2, b2, f2 = bk.get()
            for c in range(4):
                ep1 = P.op("pe", (lambda c=c: lambda e_: e_.matmul(b1[:, 0:n], lhsT=ones_b[:], rhs=buf[:, c, col0:col0 + n],
                                                                    start=(c == 0), stop=(c == 3)))(), [f1, w0, e_const], inc=(c == 3))
            for c in range(4):
                ep2 = P.op("pe", (lambda c=c: lambda e_: e_.matmul(b2[:, 0:n], lhsT=ones_b[:], rhs=sqv[:, c, 0:n],
                                                                    start=(c == 0), stop=(c == 3)))(), [ea, f2], inc=(c == 3))
            sq_free[0] = ep2
            vi, (vr, vy, vm), vfr = ver.get()
            e1 = P.op("dve", lambda e_: e_.tensor_scalar(out=vm[:, 0:n], in0=b1[:, 0:n], scalar1=1.0 / 512, scalar2=None, op0=ALU.mult), [ep1, vfr])
            bk.release(i1, e1)
            e2 = P.op("dve", lambda e_: e_.tensor_tensor(out=vy[:, 0:n], in0=vm[:, 0:n], in1=vm[:, 0:n], op=ALU.mult), [e1])
            e4 = P.op("dve", lambda e_: e_.scalar_tensor_tensor(out=vr[:, 0:n], in0=b2[:, 0:n], scalar=1.0 / 512, in1=vy[:, 0:n],
                                                                op0=ALU.mult, op1=ALU.subtract), [ep2, e2])
            bk.release(i2, e4)
            e5 = rsqrt_act(vr[:, 0:n], vy, n, 1, [e4])
            vr_x = vr
            vr = vy
            last = None
            for c in range(4):
                it, bt_, ft = bk.get()
                e6 = P.op("dve", (lambda c=c, bt_=bt_: lambda e_: e_.tensor_tensor(out=bt_[:, 0:n], in0=buf[:, c, col0:col0 + n], in1=vm[:, 0:n],
                                                                         op=ALU.subtract))(), [e1, ft, ep1])
                e7 = P.op("dve", (lambda bt_=bt_: lambda e_: e_.tensor_tensor(out=bt_[:, 0:n], in0=bt_[:, 0:n], in1=vr[:, 0:n], op=ALU.mult))(), [e6, e5])
                if func == AF.Identity:
                    e8 = P.op("dve", (lambda c=c, bt_=bt_: lambda e_: e_.tensor_scalar(out=buf[:, c, col0:col0 + n], in0=bt_[:, 0:n], scalar1=gt[:, l, c:c + 1],
                                                                             scalar2=bt[:, l, c:c + 1], op0=ALU.mult, op1=ALU.add))(), [e7, ep1, e_params])
                    rel = [e8]
                    if extra_out is not None:
                        e9 = P.op("dve", (lambda c=c, bt_=bt_: lambda e_: e_.tensor_scalar(out=extra_out[:, c, 0:n], in0=bt_[:, 0:n], scalar1=gt[:, l, c:c + 1],
                                                                                 scalar2=bt[:, l, c:c + 1], op0=ALU.mult, op1=ALU.add))(), [e7])
                        rel.append(e9)
                else:
                    e8 = P.op("act", (lambda c=c, bt_=bt_: lambda e_: e_.activation(out=buf[:, c, col0:col0 + n], in_=bt_[:, 0:n], func=func,
                                                                          scale=gt[:, l, c:c + 1], bias=bt[:, l, c:c + 1]))(), [e7, ep1, e_params])
                    rel = [e8]
                bk.release(it, rel)
                last = rel[-1]
            ver.release(vi, e7)
            return last

        def mixer(l, h):
            tiles = HALVES[h]
            base = HBASE[h]
            HL = HLEN[h]
            win = W["w_in"][l]
            gidx = l * 3 + 1
            o = 0
            cact, o = sv(o, [4, 1088], BF16)
            pact, o = sv(o, [4, 1088], BF16)
            sact, o = sv(o, [4, 1088], BF16)
            o_acts = o
            acts = [cact, pact, sact]
            S = {}

            def lcol(tt):
                t0, n = TILES[tt]
                return t0 - base, n

            def norm_step(sl, ev):
                P.barrier()
                if h == 0:
                    rmsnorm(gidx, tiles, o_acts)
            add_step([], norm_step)

            oc = 4352
            a_pad, oc = sv(oc, [4, 1054], BF16)
            a_pad_s, oc = sv(oc, [4, NSEQ, 34], BF16)
            dg0, oc = sv(oc, [CW, 128], BF16)
            dg1, oc = sv(oc, [CW, 128], BF16)
            th0, oc = sv(oc, [512], F32)
            th1, oc = sv(oc, [512], F32)
            sqv_c, oc = sv(oc, [4, 512], BF16)
            ve0, oc = sv(oc, [512], F32)
            ve0b, oc = sv(oc, [512], F32)
            ve0m, oc = sv(oc, [512], F32)
            st_a, oc = sv(oc, [4, 30], F32)
            st_as, oc = sv(oc, [4, 64], F32)
            sto0, oc = sv(oc, [512], F32)
            sto1, oc = sv(oc, [512], F32)
            thr = Ring([th0, th1])
            dgr = Ring([dg0, dg1])
            stor = Ring([sto0, sto1])
            ver_c = Ring([(ve0, ve0b, ve0m)])
            apad_ev = {}

            def conv_prep(sl, ev):
                P.barrier()
                if h == 0:
                    S["halo"] = P.op("dve", lambda e_: e_.memset(a_pad[:, :, 0:30], 0.0), [S.get("conv_rd_prev")])
                else:
                    S["halo"] = P.op("dve", lambda e_: e_.tensor_copy(out=a_pad[:, :, 0:30], in_=ahalo[:]), [S.get("conv_rd_prev")])
                    src = sconv[l].rearrange("b j c -> (b j) c")
                    evs = []
                    for r in range(4):
                        si, sv_, sfr = stor.get()
                        ed = P.dma("sp", sv_[0:120, :], src[r * 120:(r + 1) * 120, :], s_sti2[si], [sfr])
                        for bb in range(4):
                            eo_ = P.dma("pool", conv_s[l, 4 * r + bb, 0:26, :], sv_[bb * 30 + 4:bb * 30 + 30, :], s_out2[si], [ed])
                        ib, b, fb = bk.get()
                        for c in range(4):
                            ep = P.op("pe", (lambda c=c, b=b, sv_=sv_: lambda e_: e_.matmul(b[:, c * 120:(c + 1) * 120], lhsT=sv_[0:120, c * 128:(c + 1) * 128],
                                                                               rhs=ident_f[0:120, 0:120], start=True, stop=True))(), [ed, fb], inc=(c == 3))
                        for c in range(4):
                            ea = P.op("act", (lambda c=c, b=b, r=r: lambda e_: e_.activation(
                                out=a_pad_s[:, c, 4 * r:4 * r + 4, 0:30], in_=b[:, c * 120:(c + 1) * 120].rearrange("p (b j) -> p b j", b=4),
                                func=AF.Copy, scale=2.0))(), [ep])
                        bk.release(ib, ea)
                        stor.release(si, [ep, eo_])
                        evs.append(ea)
                    S["shalo"] = evs[-1]
                build_diag(0)
                return None

            add_step([], conv_prep)

            def glu_step(c):
                def f(sl, ev):
                    vv = sl[0][:, 0:1024].rearrange("p (k c) -> p k c", k=8)
                    gv = sl[0][:, 1024:2048].rearrange("p (k c) -> p k c", k=8)
                    last = None
                    for tt in tiles:
                        t0, n = TILES[tt]
                        lc, _ = lcol(tt)
                        iv, bv, fv = bk.get()
                        ig, bg, fg = bk.get()
                        for k in range(8):
                            epv = P.op("pe", (lambda k=k: lambda e_: e_.matmul(bv[:, 0:n], lhsT=vv[:, k, :], rhs=hT[:, k, t0:t0 + n],
                                                                                start=(k == 0), stop=(k == 7)))(), [ev[0], h_ev[tt], fv], inc=(k == 7))
                        for k in range(8):
                            epg = P.op("pe", (lambda k=k: lambda e_: e_.matmul(bg[:, 0:n], lhsT=gv[:, k, :], rhs=hT[:, k, t0:t0 + n],
                                                                                start=(k == 0), stop=(k == 7)))(), [fg], inc=(k == 7))
                        h_rd[tt] = epg
                        ti, thv, tfr = thr.get()
                        ea = P.op("act", lambda e_: e_.activation(out=thv[:, 0:n], in_=bg[:, 0:n], func=AF.Tanh, scale=0.5), [epg, tfr])
                        bk.release(ig, ea)
                        if tt < 4:
                            outv = a_pad[:, c, 30 + lc:30 + lc + n]
                            in0 = thv[:, 0:n]
                            in1 = bv[:, 0:n]
                        else:
                            outv = a_pad_s[:, c, :, 30:34]
                            in0 = thv[:, 0:64].rearrange("p (b t) -> p b t", t=4)
                            in1 = bv[:, 0:64].rearrange("p (b t) -> p b t", t=4)
                        ed = P.op("dve", lambda e_: e_.scalar_tensor_tensor(out=outv, in0=in0, scalar=1.0, in1=in1, op0=ALU.add, op1=ALU.mult),
                                  [ea, epv, S["halo"], S.get("shalo")])
                        if tt == 3:
                            ed = P.op("dve", lambda e_: e_.scalar_tensor_tensor(out=st_a[:, c, :], in0=thv[:, 482:512], scalar=1.0, in1=bv[:, 482:512],
                                                                                op0=ALU.add, op1=ALU.mult), [ed, S.get("st_rd")])
                        if tt == 4:
                            ed = P.op("dve", lambda e_: e_.scalar_tensor_tensor(out=st_as[:, c, :], in0=thv[:, 0:64], scalar=1.0, in1=bv[:, 0:64],
                                                                                op0=ALU.add, op1=ALU.mult), [ed, S.get("st_rd")])
                        bk.release(iv, ed)
                        thr.release(ti, ed)
                        apad_ev[(c, tt)] = ed
                        last = epg
                    return last
                return f

            for c in range(4):
                add_step([[(0, 8, 128, win[:, c * 128:(c + 1) * 128]),
                           (1024, 8, 128, win[:, 512 + c * 128:512 + (c + 1) * 128])]], glu_step(c), "glu")

            def build_diag(c):
                di, dgv, dfr = dgr.get()
                for j in range(CW):
                    edg = P.op("act", (lambda j=j: lambda e_: e_.activation(out=dgv[:, j, :], in_=ident_f[:], func=AF.Identity,
                                                                            scale=cw[:, l, c, j:j + 1]))(), [dfr, e_params, e_const])
                S["dg"] = (di, dgv, edg)

            def conv_step(c):
                def f(sl, ev):
                    di, dgv, edg = S["dg"]
                    if c < 3:
                        build_diag(c + 1)
                    last = None
                    for tt in tiles:
                        t0, n = TILES[tt]
                        lc, _ = lcol(tt)
                        ib, b, fb = bk.get()
                        deps = [edg, fb, apad_ev[(c, tt)], S.get("shalo")]
                        if tt < 4:
                            prev = tiles.index(tt) - 1
                            if prev >= 0:
                                deps.append(apad_ev[(c, tiles[prev])])
                        for j in range(CW):
                            if tt < 4:
                                rhs = a_pad[:, c, lc + j:lc + j + n]
                                outv = b[:, 0:n]
                            else:
                                rhs = a_pad_s[:, c, :, j:j + 4]
                                outv = b[:, 0:64].rearrange("p (b t) -> p b t", t=4)
                            ep = P.op("pe", (lambda j=j, rhs=rhs, outv=outv: lambda e_: e_.matmul(outv, lhsT=dgv[:, j, :], rhs=rhs,
                                                                                                  start=(j == 0), stop=(j == CW - 1)))(), deps, inc=(j == CW - 1))
                        ea = P.op("act", lambda e_: e_.activation(out=cact[:, c, lc:lc + n], in_=b[:, 0:n], func=AF.Identity,
                                                                  bias=cb[:, l, c:c + 1]), [ep, S.get("acts_rd"), e_params])
                        bk.release(ib, ea)
                        S[("cy", c, tt)] = ea
                        last = ep
                    dgr.release(di, last)
                    S["conv_rd"] = last
                    return None
                return f

            for c in range(4):
                add_step([], conv_step(c), "conv")

            def conv_ln(sl, ev):
                for tt in tiles:
                    lc, n = lcol(tt)
                    sq_w = [S[("cy", c, tt)] for c in range(4)]
                    e_last = ln_fm(cact, lc, n, clg, clb, l, AF.Silu, sqv_c, ver_c, sq_w)
                    S[("cact", tt)] = e_last
                S["conv_rd_prev"] = S["conv_rd"]
                if h == 0:
                    P.op("dve", lambda e_: e_.tensor_copy(out=ahalo[:], in_=a_pad[:, :, 1024:1054]), [apad_ev[(c_, 1)] for c_ in range(4)])
                if h == 1:
                    ib, b, fb = bk.get()
                    for c in range(4):
                        ep = P.op("pe", (lambda c=c: lambda e_: e_.matmul(b[0:30, c * 128:(c + 1) * 128], lhsT=st_a[:, c, :], rhs=ident_f[:],
                                                                           start=True, stop=True))(), [fb, apad_ev[(3, 3)]], inc=(c == 3))
                    si, so, sfr = stor.get()
                    ea = P.op("act", lambda e_: e_.activation(out=so[0:30, :], in_=b[0:30, :], func=AF.Copy, scale=0.5), [ep, sfr])
                    bk.release(ib, ea)
                    eo_ = P.dma("pool", conv_p[l], so[0:30, :], s_out2[si], [ea])
                    stor.release(si, eo_)
                    ib, b, fb = bk.get()
                    for c in range(4):
                        ep2 = P.op("pe", (lambda c=c: lambda e_: e_.matmul(b[0:64, c * 128:(c + 1) * 128], lhsT=st_as[:, c, :], rhs=ident_f[:],
                                                                            start=True, stop=True))(), [fb, apad_ev[(3, 4)]], inc=(c == 3))
                    si, so2, sfr = stor.get()
                    ea2 = P.op("act", lambda e_: e_.activation(out=so2[0:64, :], in_=b[0:64, :], func=AF.Copy, scale=0.5), [ep2, sfr])
                    bk.release(ib, ea2)
                    for bb in range(NSEQ):
                        eo2 = P.dma("pool", conv_s[l, bb, 26:30, :], so2[bb * 4:(bb + 1) * 4, :], s_out2[si], [ea2])
                    stor.release(si, eo2)
                    S["st_rd"] = [ep, ep2]
                return None

            add_step([], conv_ln)

            op_ = 8704
            pp0, op_ = sv(op_, [1040], F32)
            pp1, op_ = sv(op_, [1040], F32)
            pps, op_ = sv(op_, [4, NSEQ, 19], F32)
            ta, op_ = sv(op_, [528], F32)
            tb, op_ = sv(op_, [528], F32)
            tap, op_ = sv(op_, [528], F32)
            tbp, op_ = sv(op_, [528], F32)
            ta3, op_ = sv(op_, [NSEQ, 20], F32)
            tb3, op_ = sv(op_, [NSEQ, 20], F32)
            mt0, op_ = sv(op_, [512], BF16)
            mt1, op_ = sv(op_, [512], BF16)
            mt2, op_ = sv(op_, [512], BF16)
            st_p, op_ = sv(op_, [4, 16], F32)
            st_ps, op_ = sv(op_, [4, 64], F32)
            pst0, op_ = sv(op_, [512], F32)
            pst1, op_ = sv(op_, [512], F32)
            fix, op_ = sv(op_, [16], F32)
            ppr = Ring([pp0, pp1])
            mtr = Ring([mt0, mt1, mt2])
            pstr = Ring([pst0, pst1])

            def pool_prep(sl, ev):
                P.barrier()
                if h == 0:
                    ed = P.dma("sp", lstage[:], W["pool_w"][l].rearrange("g c d -> c g d"), s_lstage, [S.get("lstage_rd")])
                    S["pw"] = P.op("act", lambda e_: e_.activation(out=pw_b[:], in_=lstage[:], func=AF.Copy), [ed, S.get("pw_rd")])
                    G["pw"] = S["pw"]
                    G["lstage_rd"] = S["pw"]
                if h == 1:
                    src = spool[l].rearrange("b j c -> (b j) c")
                    for r in range(2):
                        si, sv_, sfr = pstr.get()
                        ed = P.dma("sp", sv_[0:120, :], src[r * 120:(r + 1) * 120, :], s_sti2[si], [sfr, S.get("conv_rd")])
                        for bb in range(8):
                            eo_ = P.dma("pool", pool_s[l, 8 * r + bb, 0:11, :], sv_[bb * 15 + 4:bb * 15 + 15, :], s_out2[si], [ed])
                        ib, b, fb = bk.get()
                        for g in range(4):
                            ep = P.op("pe", (lambda g=g, b=b, sv_=sv_: lambda e_: e_.matmul(b[:, g * 120:(g + 1) * 120], lhsT=sv_[0:120, g * 128:(g + 1) * 128],
                                                                               rhs=ident_f[0:120, 0:120], start=True, stop=True))(), [ed, fb], inc=(g == 3))
                        for g in range(4):
                            ea = P.op("act", (lambda g=g, b=b, r=r: lambda e_: e_.activation(
                                out=pps[:, g, 8 * r:8 * r + 8, 0:15], in_=b[:, g * 120:(g + 1) * 120].rearrange("p (b j) -> p b j", b=8),
                                func=AF.Copy))(), [ep])
                        bk.release(ib, ea)
                        pstr.release(si, [ep, eo_])
                        S["pps"] = ea
                return None

            add_step([], pool_prep)

            def pool_step(gp):
                def f(sl, ev):
                    wv = sl[0][:, 0:2048].rearrange("p (k c) -> p k c", k=8)
                    last = None
                    pend = []
                    for gg in range(2):
                        g = gp * 2 + gg
                        w = WIN[g]
                        nl = {2: 1, 4: 2, 8: 3, 16: 4}[w]
                        pi, pp, pfr = ppr.get()
                        if h == 0:
                            e_h = P.op("dve", lambda e_: e_.memset(pp[:, 0:15], 0.0), [pfr, S.get("conv_rd")])
                        else:
                            e_h = P.op("dve", lambda e_: e_.tensor_copy(out=pp[:, 0:15], in_=phalo[:, g, 0:15]), [pfr, S.get("conv_rd")])
                        for tt in tiles:
                            t0, n = TILES[tt]
                            lc, _ = lcol(tt)
                            ib, b, fb = bk.get()
                            for k in range(8):
                                ep = P.op("pe", (lambda k=k: lambda e_: e_.matmul(b[:, 0:n], lhsT=wv[:, k, gg * 128:(gg + 1) * 128], rhs=hT[:, k, t0:t0 + n],
                                                                                   start=(k == 0), stop=(k == 7)))(), [ev[0], h_ev[tt], fb], inc=(k == 7))
                            h_rd[tt] = ep
                            last = ep
                            if pend:
                                pend.pop(0)()
                            if tt < 4:
                                ea = P.op("act", lambda e_: e_.activation(out=pp[:, 15 + lc:15 + lc + n], in_=b[:, 0:n], func=AF.Copy), [ep, pfr, e_h])
                                bk.release(ib, ea)
                                src = pp[:, lc:lc + 15 + n]
                                L = n + 15
                                cur = src
                                weng = "pool" if (g % 2 == 0) else "dve"
                                bufs = [tap, tbp] if weng == "pool" else [ta, tb]
                                e_prev = [ea, e_h, S.get("win_rd_p" if weng == "pool" else "win_rd")]
                                sh = 1
                                lo = 0
                                for lv in range(nl):
                                    dst = bufs[lv % 2]
                                    lo = lo + sh
                                    e_prev = P.op(weng, (lambda dst=dst, cur=cur, lo=lo, sh=sh, L=L: lambda e_: e_.tensor_tensor(
                                        out=dst[:, lo:L], in0=cur[:, lo:L], in1=cur[:, lo - sh:L - sh], op=ALU.add))(), e_prev)
                                    cur = dst
                                    sh *= 2
                                mi, mv_, mfr = mtr.get()
                                em = P.op("dve", lambda e_: e_.scalar_tensor_tensor(out=mv_[:, 0:n], in0=cur[:, 15:15 + n], scalar=1.0 / w,
                                                                                    in1=src[:, 15:15 + n], op0=ALU.mult, op1=ALU.subtract), [e_prev, mfr])
                                if h == 0 and tt == 0:
                                    ef = P.op("dve", lambda e_: e_.tensor_tensor(out=fix[:, 0:w - 1], in0=cur[:, 15:15 + w - 1], in1=rc[:, 0:w - 1], op=ALU.mult), [em, e_const])
                                    em = P.op("dve", lambda e_: e_.tensor_tensor(out=mv_[:, 0:w - 1], in0=fix[:, 0:w - 1], in1=src[:, 15:15 + w - 1], op=ALU.subtract), [ef])
                                S["win_rd"] = em
                                if weng == "pool":
                                    S["win_rd_p"] = em
                            else:
                                ea = P.op("act", lambda e_: e_.activation(out=pps[:, g, :, 15:19], in_=b[:, 0:64].rearrange("p (b t) -> p b t", t=4),
                                                                          func=AF.Copy), [ep, S.get("pps")])
                                bk.release(ib, ea)
                                cur = pps[:, g]
                                bufs = [ta3, tb3]
                                e_prev = [ea, S.get("win_rd")]
                                sh = 1
                                lo = 0
                                for lv in range(nl):
                                    dst = bufs[lv % 2]
                                    lo = lo + sh
                                    e_prev = P.op("dve", (lambda dst=dst, cur=cur, lo=lo, sh=sh: lambda e_: e_.tensor_tensor(
                                        out=dst[:, :, lo:19], in0=cur[:, :, lo:19], in1=cur[:, :, lo - sh:19 - sh], op=ALU.add))(), e_prev)
                                    cur = dst
                                    sh *= 2
                                mi, mv_, mfr = mtr.get()
                                em = P.op("dve", lambda e_: e_.scalar_tensor_tensor(out=mv_[:, 0:64].rearrange("p (b t) -> p b t", t=4), in0=cur[:, :, 15:19],
                                                                                    scalar=1.0 / w, in1=pps[:, g, :, 15:19], op0=ALU.mult, op1=ALU.subtract), [e_prev, mfr])
                                S["win_rd"] = em
                                est = P.op("dve", lambda e_: e_.tensor_copy(out=st_ps[:, g, :].rearrange("p (b t) -> p b t", t=4), in_=pps[:, g, :, 15:19]), [ea, S.get("stp_rd")])
                                S["stps"] = est
                            def fin(g=g, tt=tt, n=n, lc=lc, mi=mi, mv_=mv_, em=em):
                                i2, b2, f2 = bk.get()
                                ep2 = P.op("pe", lambda e_: e_.matmul(b2[:, 0:n], lhsT=pw_b[:, g, :], rhs=mv_[:, 0:n], start=True, stop=True), [em, f2, G["pw"]])
                                mtr.release(mi, ep2)
                                G["pw_rd"] = ep2
                                ea2 = P.op("act", lambda e_: e_.activation(out=pact[:, g, lc:lc + n], in_=b2[:, 0:n], func=AF.Copy, scale=psc[:, l, g:g + 1]),
                                           [ep2, S.get("acts_rd"), e_params])
                                bk.release(i2, ea2)
                                S[("pact", tt)] = ea2
                            pend.append(fin)
                        if h == 0:
                            eh = P.op("dve", lambda e_: e_.tensor_copy(out=phalo[:, g, 0:15], in_=pp[:, 1024:1039]), [S["win_rd"]])
                            ppr.release(pi, [eh])
                        else:
                            eh = P.op("dve", lambda e_: e_.tensor_copy(out=st_p[:, g, 0:15], in_=pp[:, 1024:1039]), [S["win_rd"], S.get("stp_rd")])
                            S["stp"] = eh
                            ppr.release(pi, [eh])
                    while pend:
                        pend.pop(0)()
                    return last
                return f

            for gp in range(2):
                add_step([[(0, 8, 256, win[:, 1024 + gp * 256:1024 + (gp + 1) * 256])]], pool_step(gp), "pool")

            def pool_states(sl, ev):
                if h == 1:
                    ib, b, fb = bk.get()
                    for g in range(4):
                        ep = P.op("pe", (lambda g=g: lambda e_: e_.matmul(b[0:15, g * 128:(g + 1) * 128], lhsT=st_p[:, g, 0:15], rhs=ident_f[:],
                                                                           start=True, stop=True))(), [fb, S["stp"]], inc=(g == 3))
                    si, so, sfr = pstr.get()
                    ea = P.op("act", lambda e_: e_.activation(out=so[0:15, :], in_=b[0:15, :], func=AF.Copy), [ep, sfr])
                    bk.release(ib, ea)
                    eo_ = P.dma("pool", pool_p[l], so[0:15, :], s_out2[si], [ea])
                    pstr.release(si, eo_)
                    ib, b, fb = bk.get()
                    for g in range(4):
                        ep2 = P.op("pe", (lambda g=g: lambda e_: e_.matmul(b[0:64, g * 128:(g + 1) * 128], lhsT=st_ps[:, g, :], rhs=ident_f[:],
                                                                            start=True, stop=True))(), [fb, S["stps"]], inc=(g == 3))
                    si, so2, sfr = pstr.get()
                    ea2 = P.op("act", lambda e_: e_.activation(out=so2[0:64, :], in_=b[0:64, :], func=AF.Copy), [ep2, sfr])
                    bk.release(ib, ea2)
                    for bb in range(NSEQ):
                        eo2 = P.dma("pool", pool_s[l, bb, 11:15, :], so2[bb * 4:(bb + 1) * 4, :], s_out2[si], [ea2])
                    pstr.release(si, eo2)
                    S["stp_rd"] = [ep, ep2]
                S["pool_done"] = S["win_rd"]
                return None

            add_step([], pool_states)

            os_ = o_acts
            vn_fm2, os_ = sv(os_, [4, 528], BF16)
            vn_fm, os_ = sv(os_, [4, 528], BF16)
            vn_tok, os_ = sv(os_, [9, 512], BF16)
            sqv_s, os_ = sv(os_, [4, 512], BF16)
            ve1, os_ = sv(os_, [512], F32)
            ve1b, os_ = sv(os_, [512], F32)
            us0, os_ = sv(os_, [512], F32)
            us1, os_ = sv(os_, [512], F32)
            vn_f32, os_ = sv(os_, [4, 64], F32)
            vns, os_ = sv(os_, [512], F32)
            usr = Ring([us0, us1])
            ver_s = Ring([(ve1, ve1b, us0)])

            def sgu_prep(sl, ev):
                P.barrier()
                if h == 0:
                    ed = P.dma("sp", lstage[:], W["sgu_w"][l].rearrange("g t s -> t g s"), s_lstage, [G.get("lstage_rd")])
                    e1 = P.op("pool", lambda e_: e_.affine_select(out=lstage[:], in_=lstage[:], pattern=[[0, 4], [-1, 128]], compare_op=ALU.is_ge,
                                                                  fill=0.0, base=0, channel_multiplier=1), [ed])
                    ib, b, fb = bk.get()
                    for g in range(4):
                        ep = P.op("pe", (lambda g=g: lambda e_: e_.matmul(b[:, g * 128:(g + 1) * 128], lhsT=lstage[:, g, :], rhs=ident_f[:],
                                                                           start=True, stop=True))(), [e1, fb], inc=(g == 3))
                    ea = P.op("act", lambda e_: e_.activation(out=wct[:], in_=b[:, :].rearrange("p (g t) -> p g t", g=4), func=AF.Copy), [ep, G.get("wct_rd")])
                    bk.release(ib, ea)
                    G["wct"] = ea
                    G["lstage_rd"] = ep
                    e0 = P.op("pool", lambda e_: e_.memset(bds[:], 0.0), [G.get("bds_rd")])
                    src = W["sgu_w"][l][:, 0:4, 0:4].rearrange("g t s -> t g s")
                    for bb in range(NSEQ):
                        edb = P.dma("sp", bds[bb * 4:(bb + 1) * 4, :, bb * 4:(bb + 1) * 4], src, s_bds, [e0], allow_slow_non_contiguous=True)
                    e2 = P.op("pool", lambda e_: e_.affine_select(out=bds[:], in_=bds[:], pattern=[[0, 4], [-1, 64]], compare_op=ALU.is_ge,
                                                                  fill=0.0, base=0, channel_multiplier=1), [edb])
                    ib, b, fb = bk.get()
                    for g in range(4):
                        ep = P.op("pe", (lambda g=g: lambda e_: e_.matmul(b[0:64, g * 64:(g + 1) * 64], lhsT=bds[:, g, :], rhs=ident_f[0:64, 0:64],
                                                                           start=True, stop=True))(), [e2, fb], inc=(g == 3))
                    e3 = P.op("act", lambda e_: e_.activation(out=bd_b[:], in_=b[0:64, 0:256].rearrange("p (g t) -> p g t", g=4), func=AF.Copy), [ep, G.get("bd_rd")])
                    bk.release(ib, e3)
                    G["bd"] = e3
                    G["bds_rd"] = ep
                    eb = P.dma("sp", bs[0:1, :, 0:128], W["sgu_b"][l:l + 1], s_bs, [G.get("bs_rd")])
                    eb2 = P.op("dve", lambda e_: e_.tensor_copy(out=bs[0:1, :, 128:132], in_=bs[0:1, :, 0:4]), [eb])
                    wd_ = 4
                    while wd_ < 64:
                        eb2 = P.op("dve", (lambda wd_=wd_: lambda e_: e_.tensor_copy(out=bs[0:1, :, 128 + wd_:128 + 2 * wd_], in_=bs[0:1, :, 128:128 + wd_]))(), [eb2])
                        wd_ *= 2
                    G["bs"] = eb2
                return None

            add_step([], sgu_prep)

            def v_step(sl, ev):
                wv = [sl[i][:, 0:2048].rearrange("p (k c) -> p k c", k=8) for i in range(2)]
                last = [None]
                vbufs = [vn_fm, vn_fm2]
                vrd = [None, None]

                def emit_tr(idx, tt, vb, e_ln):
                    t0, n = TILES[tt]
                    lc, _ = lcol(tt)
                    nsub = (n + 127) // 128
                    ep = None
                    for i in range(nsub):
                        np_ = min(128, n - i * 128)
                        li = (lc // 128 + i) if tt < 4 else 8
                        ib, b, fb = bk.get()
                        for c in range(4):
                            ep = P.op("pe", (lambda c=c, i=i, np_=np_, b=b: lambda e_: e_.matmul(b[0:np_, c * 128:(c + 1) * 128], lhsT=vb[:, c, i * 128:i * 128 + np_],
                                                                                            rhs=ident_b[:], start=True, stop=True))(), [e_ln, fb, e_const], inc=(c == 3))
                        ea = P.op("act", (lambda li=li, np_=np_, b=b: lambda e_: e_.activation(out=vn_tok[0:np_, li, :], in_=b[0:np_, :], func=AF.Copy))(),
                                  [ep, S.get("vntok_rd")])
                        bk.release(ib, ea)
                        S[("vntok", li)] = ea
                    vrd[idx % 2] = ep
                    S["vnfm_rd"] = ep
                    last[0] = ep
                    if tt == 4:
                        ib, b, fb = bk.get()
                        for c in range(4):
                            ep = P.op("pe", (lambda c=c, b=b: lambda e_: e_.matmul(b[0:64, c * 128:(c + 1) * 128], lhsT=vn_f32[:, c, :], rhs=ident_f[:],
                                                                              start=True, stop=True))(), [e_ln, fb], inc=(c == 3))
                        ea = P.op("act", lambda e_: e_.activation(out=vns[0:64, :], in_=b[0:64, :], func=AF.Copy), [ep, S.get("vns_rd")])
                        bk.release(ib, ea)
                        eo_ = P.dma("pool", cv_s[l], vns[0:64, :], s_outv, [ea])
                        G["vns_rd"] = eo_
                        S["vnf32_rd"] = ep
                        last[0] = ep

                pend = None
                for idx, tt in enumerate(tiles):
                    t0, n = TILES[tt]
                    vb = vbufs[idx % 2]
                    evs = []
                    for c in range(4):
                        ib, b, fb = bk.get()
                        for k in range(8):
                            ep = P.op("pe", (lambda k=k, c=c, b=b: lambda e_: e_.matmul(b[:, 0:n], lhsT=wv[c // 2][:, k, (c % 2) * 128:(c % 2 + 1) * 128],
                                                                                   rhs=hT[:, k, t0:t0 + n], start=(k == 0), stop=(k == 7)))(),
                                      [ev[0], ev[1], h_ev[tt], fb], inc=(k == 7))
                        ea = P.op("act", (lambda c=c, b=b: lambda e_: e_.activation(out=vb[:, c, 0:n], in_=b[:, 0:n], func=AF.Copy))(),
                                  [ep, vrd[idx % 2], S.get("pool_done")])
                        bk.release(ib, ea)
                        evs.append(ea)
                        last[0] = ep
                    h_rd[tt] = ep
                    extra = vn_f32 if tt == 4 else None
                    e_ln = ln_fm(vb, 0, n, slg, slb, l, AF.Identity, sqv_s, ver_s, evs, extra_out=extra)
                    S["ln_last"] = e_ln
                    if pend is not None:
                        emit_tr(*pend)
                    pend = (idx, tt, vb, e_ln)
                emit_tr(*pend)
                return last[0]

            add_step([[(0, 8, 256, win[:, 2048:2304])], [(0, 8, 256, win[:, 2304:2560])]], v_step)

            def spatial_step(gp):
                def f(sl, ev):
                    wv = sl[0][:, 0:2048].rearrange("p (k c) -> p k c", k=8)
                    last = None
                    for gg in range(2):
                        g = gp * 2 + gg
                        for tt in tiles:
                            t0, n = TILES[tt]
                            lc, _ = lcol(tt)
                            iu, bu, fu = bk.get()
                            im, bm, fm = bk.get()
                            for k in range(8):
                                epu = P.op("pe", (lambda k=k: lambda e_: e_.matmul(bu[:, 0:n], lhsT=wv[:, k, gg * 128:(gg + 1) * 128], rhs=hT[:, k, t0:t0 + n],
                                                                                    start=(k == 0), stop=(k == 7)))(), [ev[0], h_ev[tt], fu], inc=(k == 7))
                            h_rd[tt] = epu
                            if tt < 4:
                                for i in range(4):
                                    li = lc // 128 + i
                                    P.op("pe", (lambda i=i: lambda e_: e_.matmul(bm[:, i * 128:(i + 1) * 128], lhsT=ones_f[0:1, :], rhs=bs[0:1, g, 0:128],
                                                                                  start=True, stop=False))(), [fm, G["bs"], e_const], inc=False)
                                    epm = P.op("pe", (lambda i=i, li=li: lambda e_: e_.matmul(bm[:, i * 128:(i + 1) * 128], lhsT=vn_tok[:, li, g * 128:(g + 1) * 128],
                                                                                           rhs=wct[:, g, :], start=False, stop=True))(), [S[("vntok", li)], G["wct"]], inc=(i == 3))
                            else:
                                P.op("pe", lambda e_: e_.matmul(bm[:, 0:64], lhsT=ones_f[0:1, :], rhs=bs[0:1, g, 128:192], start=True, stop=False),
                                     [fm, G["bs"], e_const], inc=False)
                                epm = P.op("pe", lambda e_: e_.matmul(bm[:, 0:64], lhsT=vn_tok[0:64, 8, g * 128:(g + 1) * 128], rhs=bd_b[:, g, :],
                                                                      start=False, stop=True), [S[("vntok", 8)], G["bd"]])
                            G["wct_rd"] = epm
                            G["bd_rd"] = epm
                            G["bs_rd"] = epm
                            S["vntok_rd"] = epm
                            ui, uv, ufr = usr.get()
                            ea = P.op("act", lambda e_: e_.activation(out=uv[:, 0:n], in_=bu[:, 0:n], func=AF.Copy), [epu, ufr, S.get("ln_last")])
                            bk.release(iu, ea)
                            ed = P.op("dve", lambda e_: e_.tensor_tensor(out=sact[:, g, lc:lc + n], in0=bm[:, 0:n], in1=uv[:, 0:n], op=ALU.mult),
                                      [ea, epm, S.get("acts_rd")])
                            bk.release(im, ed)
                            usr.release(ui, ed)
                            S[("sact", tt)] = ed
                            last = epm
                    return last
                return f

            for gp in range(2):
                add_step([[(0, 8, 256, win[:, 1536 + gp * 256:1536 + (gp + 1) * 256])]], spatial_step(gp), "spatial")

            om = o_acts
            mrg, om = sv(om, [8, 1088], BF16)
            acc, om = sv(om, [3, 512], F32)
            th2, om = sv(om, [512], F32)
            th3, om = sv(om, [512], F32)
            thr2 = Ring([th2, th3])
            wouts = [W["w_conv_out"][l], W["w_pool_out"][l], W["w_sgu_out"][l]]
            act_ready = ["cact", "pact", "sact"]

            def merge_step(m, br):
                def f(sl, ev):
                    gv = sl[0][:, 0:1024].rearrange("p (k c) -> p k c", k=8)
                    ov = sl[0][:, 1024:1536].rearrange("p (k c) -> p k c", k=4)
                    last = None
                    for ti, tt in enumerate(tiles):
                        t0, n = TILES[tt]
                        lc, _ = lcol(tt)
                        ig, bg, fg = bk.get()
                        iy, by, fy = bk.get()
                        for k in range(8):
                            epg = P.op("pe", (lambda k=k: lambda e_: e_.matmul(bg[:, 0:n], lhsT=gv[:, k, :], rhs=hT[:, k, t0:t0 + n],
                                                                                start=(k == 0), stop=(k == 7)))(), [ev[0], h_ev[tt], fg], inc=(k == 7))
                        h_rd[tt] = epg
                        for k in range(4):
                            epy = P.op("pe", (lambda k=k: lambda e_: e_.matmul(by[:, 0:n], lhsT=ov[:, k, :], rhs=acts[br][:, k, lc:lc + n],
                                                                                start=(k == 0), stop=(k == 3)))(), [fy, S[(act_ready[br], tt)]], inc=(k == 3))
                        S["acts_rd"] = epy
                        ti_, thv, tfr = thr2.get()
                        ea = P.op("act", lambda e_: e_.activation(out=thv[:, 0:n], in_=bg[:, 0:n], func=AF.Tanh, scale=0.5), [epg, tfr, S.get("sgu_tmp_rd")])
                        bk.release(ig, ea)
                        if br == 0:
                            ed = P.op("dve", lambda e_: e_.scalar_tensor_tensor(out=acc[:, ti, 0:n], in0=thv[:, 0:n], scalar=1.0, in1=by[:, 0:n],
                                                                                op0=ALU.add, op1=ALU.mult), [ea, epy, S.get("sgu_tmp_rd")])
                        else:
                            e1 = P.op("dve", lambda e_: e_.scalar_tensor_tensor(out=by[:, 0:n], in0=thv[:, 0:n], scalar=1.0, in1=by[:, 0:n],
                                                                                op0=ALU.add, op1=ALU.mult), [ea, epy])
                            if br == 1:
                                ed = P.op("dve", lambda e_: e_.tensor_tensor(out=acc[:, ti, 0:n], in0=acc[:, ti, 0:n], in1=by[:, 0:n], op=ALU.add), [e1])
                            else:
                                ed = P.op("dve", lambda e_: e_.tensor_tensor(out=mrg[:, m, lc:lc + n], in0=acc[:, ti, 0:n], in1=by[:, 0:n], op=ALU.add),
                                          [e1, S.get("mrg_rd")])
                                S[("mrg", tt)] = ed
                        bk.release(iy, ed)
                        thr2.release(ti_, ed)
                        last = epy
                    return last
                return f

            def mark_sgu_done(sl, ev):
                P.barrier()
                if h == 0:
                    rmsnorm(gidx, HALVES[1], 26880)
                else:
                    rmsnorm(l * 3 + 2, HALVES[0], 26880)
                S["sgu_tmp_rd"] = [S.get("vnfm_rd"), S.get("vntok_rd"), S.get("vnf32_rd"), G.get("vns_rd")]
                S["mrg_rd"] = S["sgu_tmp_rd"]
                return None

            add_step([], mark_sgu_done)
            for m in range(8):
                for br in range(3):
                    add_step([[(0, 8, 128, win[:, 2560 + br * 1024 + m * 128:2560 + br * 1024 + (m + 1) * 128]),
                               (1024, 4, 128, wouts[br][:, m * 128:(m + 1) * 128])]], merge_step(m, br), "merge")

            def wo_step(np2):
                def f(sl, ev):
                    wv = sl[0][:, 0:2048].rearrange("p (k c) -> p k c", k=8)
                    last = None
                    for nn in range(2):
                        nch = np2 * 2 + nn
                        for tt in tiles:
                            t0, n = TILES[tt]
                            lc, _ = lcol(tt)
                            ib, b, fb = bk.get()
                            for k in range(8):
                                ep = P.op("pe", (lambda k=k: lambda e_: e_.matmul(b[:, 0:n], lhsT=wv[:, k, nn * 128:(nn + 1) * 128], rhs=mrg[:, k, lc:lc + n],
                                                                                   start=(k == 0), stop=(k == 7)))(), [ev[0], S[("mrg", tt)], fb], inc=(k == 7))
                            ed = P.op("dve", lambda e_: e_.scalar_tensor_tensor(out=xT[:, nch, t0:t0 + n], in0=b[:, 0:n], scalar=0.5, in1=xT[:, nch, t0:t0 + n],
                                                                                op0=ALU.mult, op1=ALU.add), [ep])
                            bk.release(ib, ed)
                            x_ev[tt] = ed
                            last = ep
                    S["mrg_rd"] = last
                    G["scr_rd"] = last
                    return last
                return f

            for np2 in range(4):
                add_step([[(0, 8, 256, W["w_o"][l][:, np2 * 256:(np2 + 1) * 256])]], wo_step(np2), "wo")

        G = {}

        def final_out(sl, ev):
            P.barrier()
            o = 0
            of, o = sv(o, [8, 512], F32)
            yt0, o = sv(o, [1024], F32)
            yt1, o = sv(o, [1024], F32)
            sqv, o = sv(o, [4, 512], BF16)
            r0, o = sv(o, [512], F32)
            r1, o = sv(o, [512], F32)
            rsr = Ring([(r0, r1)])
            ytr = Ring([yt0, yt1])
            of_rd = None
            for tt in range(5):
                t0, n = TILES[tt]
                rs, ri, eo = rms_stats(tt, sqv, rsr)
                for k in range(8):
                    ed = P.op("dve", (lambda k=k: lambda e_: e_.scalar_tensor_tensor(
                        out=of[:, k, 0:n], in0=xT[:, k, t0:t0 + n], scalar=gn[:, 12, k:k + 1], in1=rs[:, 0:n], op0=ALU.mult, op1=ALU.mult))(),
                        [eo, of_rd, G.get("scr_rd")])
                rsr.release(ri, ed)
                nsub = (n + 127) // 128
                for i in range(nsub):
                    np_ = min(128, n - i * 128)
                    yi, yt, yfr = ytr.get()
                    for hf in range(2):
                        ib, b, fb = bk.get()
                        for kk in range(4):
                            k = hf * 4 + kk
                            ep = P.op("pe", (lambda kk=kk, k=k, i=i, np_=np_, b=b: lambda e_: e_.matmul(b[0:np_, kk * 128:(kk + 1) * 128], lhsT=of[:, k, i * 128:i * 128 + np_],
                                                                                                   rhs=ident_f[:], start=True, stop=True))(), [ed, fb], inc=(kk == 3))
                        ea = P.op("act", (lambda hf=hf, np_=np_, b=b, yt=yt: lambda e_: e_.activation(out=yt[0:np_, hf * 512:(hf + 1) * 512], in_=b[0:np_, :], func=AF.Copy))(),
                                  [ep, yfr])
                        bk.release(ib, ea)
                    of_rd = ep
                    eo_ = P.dma("pool", y_out[t0 + i * 128:t0 + i * 128 + np_, :], yt[0:np_, :], s_out2[yi], [ea])
                    ytr.release(yi, eo_)
            return None

        load_x()
        _par = P.q["sp"][_sp_n0:_sp_n1]
        del P.q["sp"][_sp_n0:_sp_n1]
        P.q["sp"].extend(_par)
        for l in range(depth):
            ffn(l, 1)
            for h in range(2):
                mixer(l, h)
            ffn(l, 2)
        if limit is not None:
            del steps[limit:]
        add_step([], final_out)

        slab_list = []
        for si, (slabs, comp) in enumerate(steps):
            for parts in slabs:
                slab_list.append((si, parts))
        stage_free = [None] * NS
        bf_free = [None] * NB
        slab_ready = {}
        issued = [0]

        def issue(kidx):
            si, parts = slab_list[kidx]
            ss = kidx % NS
            bs_ = kidx % NB
            tot = 0
            ed = None
            for (off, nk, ncol, src) in parts:
                dst = stg[:, ss, off:off + nk * ncol].rearrange("p (k c) -> p k c", k=nk)
                ed = P.dma("sp", dst, src.rearrange("(k p) c -> p k c", p=128), s_stage[ss], [stage_free[ss]], nobar=True)
                tot = max(tot, off + nk * ncol)
            ec = P.op("pool", lambda e_: e_.tensor_copy(out=wbf[:, bs_, 0:tot], in_=stg[:, ss, 0:tot]), [ed, bf_free[bs_]], nobar=True)
            stage_free[ss] = ec
            slab_ready[kidx] = ec

        first_slab_of_step = {}
        kk_ = 0
        for si, (slabs, comp) in enumerate(steps):
            first_slab_of_step[si] = kk_
            kk_ += len(slabs)
        nslab = len(slab_list)
        for si, (slabs, comp) in enumerate(steps):
            k0 = first_slab_of_step[si]
            want = min(nslab, k0 + NB)
            while issued[0] < want:
                issue(issued[0])
                issued[0] += 1
            sl = [wbf[:, (k0 + i) % NB, :] for i in range(len(slabs))]
            evs = [slab_ready[k0 + i] for i in range(len(slabs))]
            marks.append((getattr(comp, "__name__", "step"), len(P.q["pe"])))
            r = comp(sl, evs)
            for i in range(len(slabs)):
                assert r is not None, si
                bf_free[(k0 + i) % NB] = r

        for so_ in s_out2 + [s_outv]:
            P.op("pool", (lambda so_=so_: lambda e_: e_.wait_ge(so_[0], so_[1]))(), inc=False, nobar=True)
        if os.environ.get("DUMP_MARKS"):
            import json as _json
            _json.dump(marks, open(os.environ["DUMP_MARKS"], "w"))
        P.emit(block)
    return nc


_CACHE = {}


def _get_program(depth=DEPTH):
    if depth not in _CACHE:
        _CACHE[depth] = build_program(depth)
    return _CACHE[depth]


WEIGHT_NAMES = ["ffn1_norm", "ffn1_w_gate_up", "ffn1_w_down", "mix_norm", "w_in", "conv_dw_w", "conv_dw_b", "conv_ln_g",
                "conv_ln_b", "w_conv_out", "pool_w", "pool_scale", "w_pool_out", "sgu_ln_g", "sgu_ln_b", "sgu_w", "sgu_b",
                "w_sgu_out", "w_o", "ffn2_norm", "ffn2_w_gate_up", "ffn2_w_down", "final_norm"]


def kernel(**inputs):
    nc = _get_program(DEPTH)
    f32 = lambda a: np.ascontiguousarray(np.asarray(a, dtype=np.float32))
    xp = f32(inputs["x_prompt"])
    xs = f32(inputs["x_sample"])
    sc = f32(inputs["state_conv"])
    sp = f32(inputs["state_pool"])
    wts = {k: f32(inputs[k]) for k in WEIGHT_NAMES}
    in_maps = []
    for i in range(NCORES):
        m = dict(wts)
        m["xin"] = np.concatenate([xp[i], xs[NSEQ * i:NSEQ * (i + 1)].reshape(TS, D)], axis=0)
        m["sconv"] = np.ascontiguousarray(sc[:, NSEQ * i:NSEQ * (i + 1)])
        m["spool"] = np.ascontiguousarray(sp[:, NSEQ * i:NSEQ * (i + 1)])
        in_maps.append(m)
    res = run_bass_kernel_spmd(nc, in_maps, core_ids=list(range(NCORES)))
    R = res.results
    y_prompt = np.stack([R[i]["y"][:TP] for i in range(NCORES)], axis=0)
    y_sample = np.concatenate([R[i]["y"][TP:].reshape(NSEQ, 4, D) for i in range(NCORES)], axis=0)
    conv_prompt = np.stack([R[i]["conv_p"] for i in range(NCORES)], axis=1)
    conv_sample = np.concatenate([R[i]["conv_s"] for i in range(NCORES)], axis=1)
    pool_prompt = np.stack([R[i]["pool_p"] for i in range(NCORES)], axis=1)
    pool_sample = np.concatenate([R[i]["pool_s"] for i in range(NCORES)], axis=1)
    chunk_v = np.concatenate([R[i]["cv_s"].reshape(DEPTH, NSEQ, 4, 512) for i in range(NCORES)], axis=1)
    return (y_prompt, y_sample, conv_prompt, conv_sample, pool_prompt, pool_sample, chunk_v)
```

```python
import numpy as np
from contextlib import ExitStack
import concourse.bass as bass
import concourse.mybir as mybir
from concourse.bass_utils import run_bass_kernel_spmd

F32 = mybir.dt.float32
BF16 = mybir.dt.bfloat16
ALU = mybir.AluOpType
AF = mybir.ActivationFunctionType

import os
DBG_V = int(os.environ.get("DBG_V", "0"))
NCORES = 8
DEPTH = 4
D = 1024
DFF = 2816
DIN = 5632
TP = 2048
TS = 64
T = TP + TS
NSEQ = 16
TILES = [(0, 512), (512, 512), (1024, 512), (1536, 512), (2048, 64)]
HALVES = [[0, 1], [2, 3, 4]]
HBASE = [0, 1024]
HLEN = [1024, 1088]
CW = 31
PB = 15
WIN = (2, 4, 8, 16)
RMS_EPS = 1e-6
LN_EPS = 1e-5
SLAB = 2048
NS = 2
NB = 3
LA = 2
HG = [(0, 11), (11, 11)]


class _Rec:
    def __init__(self):
        self.call = None

    def __getattr__(self, name):
        def f(*a, **kw):
            self.call = (name, a, kw)
            return None
        return f


class Prog:
    ENG = ("pe", "act", "dve", "pool", "sp")

    def __init__(self, nc, stack):
        self.nc = nc
        self.stack = stack
        self.q = {e: [] for e in self.ENG}
        self.sem = {e: stack.enter_context(nc.semaphore("s_" + e)) for e in self.ENG}
        self.cnt = {e: 0 for e in self.ENG}
        self.base = []
        self.dma_sems = []

    def barrier(self):
        b = [(self.sem[e], self.cnt[e]) for e in ("pe", "act", "dve") if self.cnt[e] > 0]
        b += [(s[0], s[1]) for s in self.dma_sems if s[1] > 0]
        self.base = b

    def op(self, eng, fn, waits=(), inc=True, nobar=False):
        ev = None
        if inc:
            self.cnt[eng] += 1
            ev = (self.sem[eng], self.cnt[eng])
        w = flat(waits)
        if not nobar:
            w = w + self.base
        rec = _Rec()
        fn(rec)
        name, a, kw = rec.call
        self.q[eng].append(((lambda e, name=name, a=a, kw=kw: getattr(e, name)(*a, **kw)), w, ev, 1))
        return ev

    def newsem(self, name):
        return [self.stack.enter_context(self.nc.semaphore(name)), 0]

    def dma(self, eng, out, in_, sem, waits=(), nobar=False, **kw):
        sem[1] += 16
        ev = (sem[0], sem[1])
        w = flat(waits)
        if not nobar:
            w = w + self.base
        self.q[eng].append((lambda e: e.dma_start(out=out, in_=in_, **kw), w, ev, 16))
        return ev

    def emit(self, block):
        engs = {"pe": block.tensor, "act": block.scalar, "dve": block.vector,
                "pool": block.gpsimd, "sp": block.sync}
        for ename, deco in engs.items():
            ops = self.q[ename]

            def body(e, ops=ops):
                seen = {}
                for fn, waits, ev, incv in ops:
                    for (s, v) in waits:
                        if seen.get(s.name, 0) >= v:
                            continue
                        seen[s.name] = v
                        e.wait_ge(s, v)
                    ins = fn(e)
                    if ev is not None:
                        ins.then_inc(ev[0], incv)
            deco(body)


def flat(w):
    out = []
    if w is None:
        return out
    if isinstance(w, tuple) and len(w) == 2 and not isinstance(w[0], (tuple, list)) and w[0] is not None and not isinstance(w[1], (tuple, list)):
        return [w]
    for x in w:
        out.extend(flat(x))
    return out


class Ring:
    def __init__(self, bufs):
        self.bufs = bufs
        self.free = [None] * len(bufs)
        self.nxt = 0

    def get(self):
        i = self.nxt
        self.nxt = (i + 1) % len(self.bufs)
        return i, self.bufs[i], self.free[i]

    def release(self, i, evs):
        self.free[i] = evs


def build_program(depth=DEPTH, limit=None):
    nc = bass.Bass("TRN2", target_bir_lowering=False)

    def din(name, shape):
        return nc.dram_tensor(name, list(shape), F32, kind="ExternalInput").ap()

    def dout(name, shape):
        return nc.dram_tensor(name, list(shape), F32, kind="ExternalOutput").ap()

    xin = din("xin", [T, D])
    sconv = din("sconv", [DEPTH, NSEQ, 30, 512])
    spool = din("spool", [DEPTH, NSEQ, 15, 512])
    W = {}
    for name, shape in [
        ("ffn1_norm", [DEPTH, D]), ("ffn1_w_gate_up", [DEPTH, D, 2 * DFF]), ("ffn1_w_down", [DEPTH, DFF, D]),
        ("mix_norm", [DEPTH, D]), ("w_in", [DEPTH, D, DIN]), ("conv_dw_w", [DEPTH, CW, 512]),
        ("conv_dw_b", [DEPTH, 512]), ("conv_ln_g", [DEPTH, 512]), ("conv_ln_b", [DEPTH, 512]),
        ("w_conv_out", [DEPTH, 512, D]), ("pool_w", [DEPTH, 4, 128, 128]), ("pool_scale", [DEPTH, 512]),
        ("w_pool_out", [DEPTH, 512, D]), ("sgu_ln_g", [DEPTH, 512]), ("sgu_ln_b", [DEPTH, 512]),
        ("sgu_w", [DEPTH, 4, 128, 128]), ("sgu_b", [DEPTH, 4, 128]), ("w_sgu_out", [DEPTH, 512, D]),
        ("w_o", [DEPTH, D, D]), ("ffn2_norm", [DEPTH, D]), ("ffn2_w_gate_up", [DEPTH, D, 2 * DFF]),
        ("ffn2_w_down", [DEPTH, DFF, D]), ("final_norm", [D]),
    ]:
        W[name] = din(name, shape)
    y_out = dout("y", [T, D])
    conv_p = dout("conv_p", [DEPTH, 30, 512])
    conv_s = dout("conv_s", [DEPTH, NSEQ, 30, 512])
    pool_p = dout("pool_p", [DEPTH, 15, 512])
    pool_s = dout("pool_s", [DEPTH, NSEQ, 15, 512])
    cv_s = dout("cv_s", [DEPTH, TS, 512])

    st = ExitStack()
    with st:
        def sb(name, shape, dt):
            return st.enter_context(nc.sbuf_tensor(name, list(shape), dt))

        xT = sb("xT", [128, 8, T], F32)
        hT = sb("hT", [128, 8, T], BF16)
        stg = sb("stg", [128, NS, SLAB], F32)
        wbf = sb("wbf", [128, NB, SLAB], BF16)
        ident_f = sb("ident_f", [128, 128], F32)
        ident_b = sb("ident_b", [128, 128], BF16)
        ones_b = sb("ones_b", [128, 128], BF16)
        ones_f = sb("ones_f", [128, 128], F32)
        neghalf = sb("neghalf", [128, 512], F32)
        rc = sb("rc", [128, 16], F32)
        epsc = sb("epsc", [128, 2], F32)
        gn = sb("gn", [128, 13, 8], F32)
        cw = sb("cw", [128, DEPTH, 4, CW], F32)
        cb = sb("cb", [128, DEPTH, 4], F32)
        clg = sb("clg", [128, DEPTH, 4], F32)
        clb = sb("clb", [128, DEPTH, 4], F32)
        psc = sb("psc", [128, DEPTH, 4], F32)
        slg = sb("slg", [128, DEPTH, 4], F32)
        slb = sb("slb", [128, DEPTH, 4], F32)
        lstage = sb("lstage", [128, 4, 128], F32)
        pw_b = sb("pw_b", [128, 4, 128], BF16)
        wct = sb("wct", [128, 4, 128], BF16)
        bds = sb("bds", [64, 4, 64], F32)
        bd_b = sb("bd_b", [64, 4, 64], BF16)
        bs = sb("bs", [1, 4, 192], F32)
        ahalo = sb("ahalo", [128, 4, 30], BF16)
        phalo = sb("phalo", [128, 4, 16], F32)
        NSCR = 29952
        scr = sb("scr", [128, NSCR], BF16)
        banks = [st.enter_context(nc.psum_tensor("bk%d" % i, [128, 512], F32)) for i in range(8)]

        P = Prog(nc, st)
        block = st.enter_context(nc.Block())
        bk = Ring(banks)

        def sv(off, shape, dt):
            n = int(np.prod(shape))
            if dt == F32:
                assert off % 2 == 0
                v = scr[:, off:off + 2 * n].bitcast(F32)
                end = off + 2 * n
            else:
                v = scr[:, off:off + n]
                end = off + n
            assert end <= NSCR, (off, shape, end)
            if len(shape) == 1:
                return v, end
            names = " ".join("d%d" % i for i in range(len(shape)))
            kw = {"d%d" % i: shape[i] for i in range(len(shape) - 1)}
            return v.rearrange("p (%s) -> p %s" % (names, names), **kw), end

        def bcast_col(col, n):
            return bass.AP(col.tensor, col.offset, [list(col.ap[0]), [0, n]])

        s_stage = [P.newsem("stage%d" % i) for i in range(NS)]
        s_in2 = [P.newsem("s_in%d" % i) for i in range(2)]
        s_par = P.newsem("s_par")
        s_lstage = P.newsem("s_lstage")
        s_bds = P.newsem("s_bds")
        s_bs = P.newsem("s_bs")
        s_out2 = [P.newsem("s_out%d" % i) for i in range(2)]
        s_outv = P.newsem("s_outv")
        s_sti2 = [P.newsem("s_sti%d" % i) for i in range(2)]
        P.dma_sems = s_in2 + [s_lstage, s_bds, s_bs, s_outv] + s_out2 + s_sti2

        e = P.op("pool", lambda e_: e_.memset(ident_f[:], 1.0))
        e_ident = P.op("pool", lambda e_: e_.affine_select(out=ident_f[:], in_=ident_f[:], pattern=[[-1, 128]],
                                                            compare_op=ALU.is_equal, fill=0.0, base=0, channel_multiplier=1), [e])
        e_identb = P.op("dve", lambda e_: e_.tensor_copy(out=ident_b[:], in_=ident_f[:]), [e_ident])
        e_onesb = P.op("dve", lambda e_: e_.memset(ones_b[:], 1.0))
        e_onesf = P.op("dve", lambda e_: e_.memset(ones_f[:], 1.0))
        e_nh = P.op("pool", lambda e_: e_.memset(neghalf[:], -0.5))
        for t_ in range(16):
            e_rc = P.op("dve", (lambda t_: lambda e_: e_.memset(rc[:, t_:t_ + 1], 1.0 / (t_ + 1)))(t_))
        e_eps0 = P.op("dve", lambda e_: e_.memset(epsc[:, 0:1], float(D * RMS_EPS)))
        e_eps1 = P.op("dve", lambda e_: e_.memset(epsc[:, 1:2], float(LN_EPS)))
        e_const = [e_ident, e_identb, e_onesb, e_onesf, e_nh, e_rc, e_eps0, e_eps1]

        _sp_n0 = len(P.q["sp"])
        gn4 = gn[:, 0:12, :].rearrange("p (l j) k -> p l j k", j=3)
        pe_ = []
        for j, nm in enumerate(["ffn1_norm", "mix_norm", "ffn2_norm"]):
            for l in range(DEPTH):
                pe_.append(P.dma("sp", gn[:, 3 * l + j, :], W[nm][l].rearrange("(k p) -> p k", p=128), s_par, allow_slow_non_contiguous=True))
        pe_.append(P.dma("sp", gn[:, 12, :], W["final_norm"].rearrange("(k p) -> p k", p=128), s_par, allow_slow_non_contiguous=True))
        for l in range(DEPTH):
            for c_ in range(4):
                pe_.append(P.dma("sp", cw[:, l, c_, :], W["conv_dw_w"][l][:, c_ * 128:(c_ + 1) * 128].rearrange("j p -> p j"), s_par, allow_slow_non_contiguous=True))
        for tile_, nm in [(cb, "conv_dw_b"), (clg, "conv_ln_g"), (clb, "conv_ln_b"), (psc, "pool_scale"), (slg, "sgu_ln_g"), (slb, "sgu_ln_b")]:
            pe_.append(P.dma("sp", tile_[:], W[nm].rearrange("l (c p) -> p l c", p=128), s_par, allow_slow_non_contiguous=True))
        e_par = pe_[-1]
        e_gn = P.op("dve", lambda e_: e_.tensor_scalar(out=gn[:], in0=gn[:], scalar1=32.0, scalar2=None, op0=ALU.mult), [e_par])
        e_cw = P.op("dve", lambda e_: e_.tensor_scalar(out=cw[:], in0=cw[:], scalar1=0.5, scalar2=None, op0=ALU.mult), [e_par])
        e_params = [e_par, e_gn, e_cw]
        _sp_n1 = len(P.q["sp"])

        x_ev = [None] * 5
        h_ev = [None] * 5
        h_rd = [None] * 5
        out_evs = []

        steps = []

        def add_step(slabs, compute, name=None):
            if name is not None:
                compute.__name__ = name
            steps.append((slabs, compute))
        marks = []

        def load_x():
            xs0, o = sv(0, [1024], F32)
            xs1, o = sv(o, [1024], F32)
            xring = Ring([xs0, xs1])
            last = {}
            for s_ in range(17):
                tok0 = s_ * 128
                n = 128 if s_ < 16 else 64
                tt = min(s_ // 4, 4)
                i, xs, fr = xring.get()
                ed = P.dma("sp", xs[0:n, :], xin[tok0:tok0 + n, :], s_in2[i], [fr])
                evs = []
                for hf in range(2):
                    bi, b, bfr = bk.get()
                    for kk in range(4):
                        k = hf * 4 + kk
                        ep = P.op("pe", (lambda b=b, kk=kk, xs=xs, k=k, n=n: lambda e_: e_.matmul(
                            b[:, kk * 128:kk * 128 + n], lhsT=xs[0:n, k * 128:(k + 1) * 128], rhs=ident_f[0:n, 0:n],
                            start=True, stop=True))(), [ed, bfr, e_const], inc=(kk == 3))
                    src = b[:, :].rearrange("p (a c) -> p a c", a=4)[:, :, 0:n]
                    ec = P.op("act", (lambda src=src, hf=hf, tok0=tok0, n=n: lambda e_: e_.activation(
                        out=xT[:, hf * 4:(hf + 1) * 4, tok0:tok0 + n], in_=src, func=AF.Copy))(), [ep])
                    bk.release(bi, ec)
                    evs.append(ep)
                    x_ev[tt] = ec
                xring.release(i, evs[-1])

        I32 = mybir.dt.int32

        def rsqrt_nr(xs, ys, n, waits):
            xi = xs[:, 0:n].bitcast(I32)
            yi = ys[:, 0:n].bitcast(I32)
            e = P.op("dve", lambda e_: e_.tensor_scalar(out=yi, in0=xi, scalar1=1, scalar2=None, op0=ALU.arith_shift_right), waits)
            e = P.op("dve", lambda e_: e_.tensor_scalar(out=yi, in0=yi, scalar1=-1, scalar2=0x5f3759df, op0=ALU.mult, op1=ALU.add), [e])
            ti, tb, tf = bk.get()
            for it in range(2):
                e = P.op("dve", lambda e_: e_.tensor_tensor(out=tb[:, 0:n], in0=ys[:, 0:n], in1=ys[:, 0:n], op=ALU.mult), [e, tf])
                e = P.op("dve", lambda e_: e_.tensor_tensor(out=tb[:, 0:n], in0=tb[:, 0:n], in1=xs[:, 0:n], op=ALU.mult), [e])
                e = P.op("dve", lambda e_: e_.tensor_scalar(out=tb[:, 0:n], in0=tb[:, 0:n], scalar1=-0.5, scalar2=1.5, op0=ALU.mult, op1=ALU.add), [e])
                e = P.op("dve", lambda e_: e_.tensor_tensor(out=ys[:, 0:n], in0=ys[:, 0:n], in1=tb[:, 0:n], op=ALU.mult), [e])
            bk.release(ti, e)
            return e

        def rsqrt_act(src, ys, n, bias_col, waits):
            e = P.op("act", lambda e_: e_.activation(out=ys[:, 0:n], in_=src, func=AF.Ln, bias=epsc[:, bias_col:bias_col + 1]), [waits, e_const])
            e = P.op("act", lambda e_: e_.activation(out=ys[:, 0:n], in_=ys[:, 0:n], func=AF.Exp, scale=-0.5), [e])
            return e

        def rms_stats(tt, sqv, rsr):
            t0, n = TILES[tt]
            bi, b, bfr = bk.get()
            for hf in range(2):
                ea = P.op("act", (lambda hf=hf: lambda e_: e_.activation(
                    out=sqv[:, :, 0:n], in_=xT[:, hf * 4:(hf + 1) * 4, t0:t0 + n], func=AF.Square))(),
                    [x_ev[tt], sq_free[0]])
                for kk in range(4):
                    ep = P.op("pe", (lambda kk=kk, hf=hf: lambda e_: e_.matmul(
                        b[:, 0:n], lhsT=ones_b[:], rhs=sqv[:, kk, 0:n], start=(hf == 0 and kk == 0),
                        stop=(hf == 1 and kk == 3)))(), [ea, bfr, e_const], inc=(kk == 3))
                sq_free[0] = ep
            ri, rs, rfr = rsr.get()
            xs_, ys_ = rs
            eo = rsqrt_act(b[:, 0:n], ys_, n, 0, [ep, rfr])
            bk.release(bi, eo)
            rs = ys_
            return rs, ri, eo

        sq_free = [None]

        def rmsnorm(gidx, tiles, sq_off):
            sqv, o = sv(sq_off, [4, 512], BF16)
            r1, o = sv(o, [512], F32)
            rsr = Ring([(None, r1)])
            for tt in tiles:
                t0, n = TILES[tt]
                rs, ri, eo = rms_stats(tt, sqv, rsr)
                for k in range(8):
                    ed = P.op("dve", (lambda k=k: lambda e_: e_.scalar_tensor_tensor(
                        out=hT[:, k, t0:t0 + n], in0=xT[:, k, t0:t0 + n], scalar=gn[:, gidx, k:k + 1],
                        in1=rs[:, 0:n], op0=ALU.mult, op1=ALU.mult))(), [eo, h_rd[tt], e_params])
                rsr.release(ri, ed)
                h_ev[tt] = ed
            return o

        def ffn(l, which):
            wgu = W["ffn%d_w_gate_up" % which][l]
            wd = W["ffn%d_w_down" % which][l]
            gidx = l * 3 + (0 if which == 1 else 2)
            hid, o = sv(0, [11, T], BF16)
            sg0, o2 = sv(o, [512], F32)
            sg1, o2 = sv(o2, [512], F32)
            sgr = Ring([sg0, sg1])
            hid_ev = [None] * 5

            def norm_step(sl, ev):
                if which == 1 and l > 0:
                    P.base = []
                else:
                    P.barrier()
                rmsnorm(gidx, range(5) if which == 1 else [2, 3, 4], o2)
            add_step([], norm_step)

            def up_step(j):
                def f(sl, ev):
                    gv = sl[0][:, 0:1024].rearrange("p (k c) -> p k c", k=8)
                    uv = sl[0][:, 1024:2048].rearrange("p (k c) -> p k c", k=8)
                    last = None
                    for tt in range(5):
                        t0, n = TILES[tt]
                        ia, ba, fa = bk.get()
                        ib, bb, fb = bk.get()
                        for k in range(8):
                            epa = P.op("pe", (lambda k=k: lambda e_: e_.matmul(ba[:, 0:n], lhsT=gv[:, k, :], rhs=hT[:, k, t0:t0 + n],
                                                                                start=(k == 0), stop=(k == 7)))(), [ev[0], h_ev[tt], fa], inc=(k == 7))
                        for k in range(8):
                            epb = P.op("pe", (lambda k=k: lambda e_: e_.matmul(bb[:, 0:n], lhsT=uv[:, k, :], rhs=hT[:, k, t0:t0 + n],
                                                                                start=(k == 0), stop=(k == 7)))(), [fb], inc=(k == 7))
                        h_rd[tt] = epb
                        si, sgv, sfr = sgr.get()
                        ea = P.op("act", lambda e_: e_.activation(out=sgv[:, 0:n], in_=ba[:, 0:n], func=AF.Silu), [epa, sfr])
                        bk.release(ia, ea)
                        ed = P.op("dve", lambda e_: e_.tensor_tensor(out=hid[:, j, t0:t0 + n], in0=bb[:, 0:n], in1=sgv[:, 0:n], op=ALU.mult), [ea, epb])
                        bk.release(ib, ed)
                        sgr.release(si, ed)
                        hid_ev[tt] = ed
                        last = epb
                    return last
                return f

            def down_step(nk, m):
                def f(sl, ev):
                    wv = sl[0][:, 0:nk * 128].rearrange("p (k c) -> p k c", k=nk)
                    last = None
                    for tt in range(5):
                        t0, n = TILES[tt]
                        ib, b, fb = bk.get()
                        for k in range(nk):
                            ep = P.op("pe", (lambda k=k: lambda e_: e_.matmul(b[:, 0:n], lhsT=wv[:, k, :], rhs=hid[:, k, t0:t0 + n],
                                                                               start=(k == 0), stop=(k == nk - 1)))(), [ev[0], hid_ev[tt], fb], inc=(k == nk - 1))
                        ed = P.op("dve", lambda e_: e_.scalar_tensor_tensor(out=xT[:, m, t0:t0 + n], in0=b[:, 0:n], scalar=0.5,
                                                                            in1=xT[:, m, t0:t0 + n], op0=ALU.mult, op1=ALU.add), [ep])
                        bk.release(ib, ed)
                        x_ev[tt] = ed
                        last = ep
                    return last
                return f

            for (c0, ncn) in HG:
                for j in range(ncn):
                    c = c0 + j
                    add_step([[(0, 8, 128, wgu[:, c * 128:(c + 1) * 128]),
                               (1024, 8, 128, wgu[:, DFF + c * 128:DFF + (c + 1) * 128])]], up_step(j), "ffn_up")
                for m in range(8):
                    add_step([[(0, ncn, 128, wd[c0 * 128:(c0 + ncn) * 128, m * 128:(m + 1) * 128])]], down_step(ncn, m), "ffn_down")

        def ln_fm(buf, col0, n, gt, bt, l, func, sqv, ver, w0, extra_out=None):
            ea = P.op("act", lambda e_: e_.activation(out=sqv[:, :, 0:n], in_=buf[:, :, col0:col0 + n], func=AF.Square), [sq_free[0], w0])
            i1, b1, f1 = bk.get()
            i2, b2, f2 = bk.get()
            for c in range(4):
                ep1 = P.op("pe", (lambda c=c: lambda e_: e_.matmul(b1[:, 0:n], lhsT=ones_b[:], rhs=buf[:, c, col0:col0 + n],
                                                                    start=(c == 0), stop=(c == 3)))(), [f1, w0, e_const], inc=(c == 3))
            for c in range(4):
                ep2 = P.op("pe", (lambda c=c: lambda e_: e_.matmul(b2[:, 0:n], lhsT=ones_b[:], rhs=sqv[:, c, 0:n],
                                                                    start=(c == 0), stop=(c == 3)))(), [ea, f2], inc=(c == 3))
            sq_free[0] = ep2
            vi, (vr, vy, vm), vfr = ver.get()
            e1 = P.op("dve", lambda e_: e_.tensor_scalar(out=vm[:, 0:n], in0=b1[:, 0:n], scalar1=1.0 / 512, scalar2=None, op0=ALU.mult), [ep1, vfr])
            bk.release(i1, e1)
            e2 = P.op("dve", lambda e_: e_.tensor_tensor(out=vy[:, 0:n], in0=vm[:, 0:n], in1=vm[:, 0:n], op=ALU.mult), [e1])
            e4 = P.op("dve", lambda e_: e_.scalar_tensor_tensor(out=vr[:, 0:n], in0=b2[:, 0:n], scalar=1.0 / 512, in1=vy[:, 0:n],
                                                                op0=ALU.mult, op1=ALU.subtract), [ep2, e2])
            bk.release(i2, e4)
            e5 = rsqrt_act(vr[:, 0:n], vy, n, 1, [e4])
            vr_x = vr
            vr = vy
            last = None
            for c in range(4):
                it, bt_, ft = bk.get()
                e6 = P.op("dve", (lambda c=c, bt_=bt_: lambda e_: e_.tensor_tensor(out=bt_[:, 0:n], in0=buf[:, c, col0:col0 + n], in1=vm[:, 0:n],
                                                                         op=ALU.subtract))(), [e1, ft, ep1])
                e7 = P.op("dve", (lambda bt_=bt_: lambda e_: e_.tensor_tensor(out=bt_[:, 0:n], in0=bt_[:, 0:n], in1=vr[:, 0:n], op=ALU.mult))(), [e6, e5])
                if func == AF.Identity:
                    e8 = P.op("dve", (lambda c=c, bt_=bt_: lambda e_: e_.tensor_scalar(out=buf[:, c, col0:col0 + n], in0=bt_[:, 0:n], scalar1=gt[:, l, c:c + 1],
                                                                             scalar2=bt[:, l, c:c + 1], op0=ALU.mult, op1=ALU.add))(), [e7, ep1, e_params])
                    rel = [e8]
                    if extra_out is not None:
                        e9 = P.op("dve", (lambda c=c, bt_=bt_: lambda e_: e_.tensor_scalar(out=extra_out[:, c, 0:n], in0=bt_[:, 0:n], scalar1=gt[:, l, c:c + 1],
                                                                                 scalar2=bt[:, l, c:c + 1], op0=ALU.mult, op1=ALU.add))(), [e7])
                        rel.append(e9)
                else:
                    e8 = P.op("act", (lambda c=c, bt_=bt_: lambda e_: e_.activation(out=buf[:, c, col0:col0 + n], in_=bt_[:, 0:n], func=func,
                                                                          scale=gt[:, l, c:c + 1], bias=bt[:, l, c:c + 1]))(), [e7, ep1, e_params])
                    rel = [e8]
                bk.release(it, rel)
                last = rel[-1]
            ver.release(vi, e7)
            return last

        def mixer(l, h):
            tiles = HALVES[h]
            base = HBASE[h]
            HL = HLEN[h]
            win = W["w_in"][l]
            gidx = l * 3 + 1
            o = 0
            cact, o = sv(o, [4, 1088], BF16)
            pact, o = sv(o, [4, 1088], BF16)
            sact, o = sv(o, [4, 1088], BF16)
            o_acts = o
            acts = [cact, pact, sact]
            S = {}

            def lcol(tt):
                t0, n = TILES[tt]
                return t0 - base, n

            def norm_step(sl, ev):
                P.barrier()
                if h == 0:
                    rmsnorm(gidx, tiles, o_acts)
            add_step([], norm_step)

            oc = 4352
            a_pad, oc = sv(oc, [4, 1054], BF16)
            a_pad_s, oc = sv(oc, [4, NSEQ, 34], BF16)
            dg0, oc = sv(oc, [CW, 128], BF16)
            dg1, oc = sv(oc, [CW, 128], BF16)
            th0, oc = sv(oc, [512], F32)
            th1, oc = sv(oc, [512], F32)
            sqv_c, oc = sv(oc, [4, 512], BF16)
            ve0, oc = sv(oc, [512], F32)
            ve0b, oc = sv(oc, [512], F32)
            ve0m, oc = sv(oc, [512], F32)
            st_a, oc = sv(oc, [4, 30], F32)
            st_as, oc = sv(oc, [4, 64], F32)
            sto0, oc = sv(oc, [512], F32)
            sto1, oc = sv(oc, [512], F32)
            thr = Ring([th0, th1])
            dgr = Ring([dg0, dg1])
            stor = Ring([sto0, sto1])
            ver_c = Ring([(ve0, ve0b, ve0m)])
            apad_ev = {}

            def conv_prep(sl, ev):
                P.barrier()
                if h == 0:
                    S["halo"] = P.op("dve", lambda e_: e_.memset(a_pad[:, :, 0:30], 0.0), [S.get("conv_rd_prev")])
                else:
                    S["halo"] = P.op("dve", lambda e_: e_.tensor_copy(out=a_pad[:, :, 0:30], in_=ahalo[:]), [S.get("conv_rd_prev")])
                    src = sconv[l].rearrange("b j c -> (b j) c")
                    evs = []
                    for r in range(4):
                        si, sv_, sfr = stor.get()
                        ed = P.dma("sp", sv_[0:120, :], src[r * 120:(r + 1) * 120, :], s_sti2[si], [sfr])
                        for bb in range(4):
                            eo_ = P.dma("pool", conv_s[l, 4 * r + bb, 0:26, :], sv_[bb * 30 + 4:bb * 30 + 30, :], s_out2[si], [ed])
                        ib, b, fb = bk.get()
                        for c in range(4):
                            ep = P.op("pe", (lambda c=c, b=b, sv_=sv_: lambda e_: e_.matmul(b[:, c * 120:(c + 1) * 120], lhsT=sv_[0:120, c * 128:(c + 1) * 128],
                                                                               rhs=ident_f[0:120, 0:120], start=True, stop=True))(), [ed, fb], inc=(c == 3))
                        for c in range(4):
                            ea = P.op("act", (lambda c=c, b=b, r=r: lambda e_: e_.activation(
                                out=a_pad_s[:, c, 4 * r:4 * r + 4, 0:30], in_=b[:, c * 120:(c + 1) * 120].rearrange("p (b j) -> p b j", b=4),
                                func=AF.Copy, scale=2.0))(), [ep])
                        bk.release(ib, ea)
                        stor.release(si, [ep, eo_])
                        evs.append(ea)
                    S["shalo"] = evs[-1]
                build_diag(0)
                return None

            add_step([], conv_prep)

            def glu_step(c):
                def f(sl, ev):
                    vv = sl[0][:, 0:1024].rearrange("p (k c) -> p k c", k=8)
                    gv = sl[0][:, 1024:2048].rearrange("p (k c) -> p k c", k=8)
                    last = None
                    for tt in tiles:
                        t0, n = TILES[tt]
                        lc, _ = lcol(tt)
                        iv, bv, fv = bk.get()
                        ig, bg, fg = bk.get()
                        for k in range(8):
                            epv = P.op("pe", (lambda k=k: lambda e_: e_.matmul(bv[:, 0:n], lhsT=vv[:, k, :], rhs=hT[:, k, t0:t0 + n],
                                                                                start=(k == 0), stop=(k == 7)))(), [ev[0], h_ev[tt], fv], inc=(k == 7))
                        for k in range(8):
                            epg = P.op("pe", (lambda k=k: lambda e_: e_.matmul(bg[:, 0:n], lhsT=gv[:, k, :], rhs=hT[:, k, t0:t0 + n],
                                                                                start=(k == 0), stop=(k == 7)))(), [fg], inc=(k == 7))
                        h_rd[tt] = epg
                        ti, thv, tfr = thr.get()
                        ea = P.op("act", lambda e_: e_.activation(out=thv[:, 0:n], in_=bg[:, 0:n], func=AF.Tanh, scale=0.5), [epg, tfr])
                        bk.release(ig, ea)
                        if tt < 4:
                            outv = a_pad[:, c, 30 + lc:30 + lc + n]
                            in0 = thv[:, 0:n]
                            in1 = bv[:, 0:n]
                        else:
                            outv = a_pad_s[:, c, :, 30:34]
                            in0 = thv[:, 0:64].rearrange("p (b t) -> p b t", t=4)
                            in1 = bv[:, 0:64].rearrange("p (b t) -> p b t", t=4)
                        ed = P.op("dve", lambda e_: e_.scalar_tensor_tensor(out=outv, in0=in0, scalar=1.0, in1=in1, op0=ALU.add, op1=ALU.mult),
                                  [ea, epv, S["halo"], S.get("shalo")])
                        if tt == 3:
                            ed = P.op("dve", lambda e_: e_.scalar_tensor_tensor(out=st_a[:, c, :], in0=thv[:, 482:512], scalar=1.0, in1=bv[:, 482:512],
                                                                                op0=ALU.add, op1=ALU.mult), [ed, S.get("st_rd")])
                        if tt == 4:
                            ed = P.op("dve", lambda e_: e_.scalar_tensor_tensor(out=st_as[:, c, :], in0=thv[:, 0:64], scalar=1.0, in1=bv[:, 0:64],
                                                                                op0=ALU.add, op1=ALU.mult), [ed, S.get("st_rd")])
                        bk.release(iv, ed)
                        thr.release(ti, ed)
                        apad_ev[(c, tt)] = ed
                        last = epg
                    return last
                return f

            for c in range(4):
                add_step([[(0, 8, 128, win[:, c * 128:(c + 1) * 128]),
                           (1024, 8, 128, win[:, 512 + c * 128:512 + (c + 1) * 128])]], glu_step(c), "glu")

            def build_diag(c):
                di, dgv, dfr = dgr.get()
                for j in range(CW):
                    edg = P.op("act", (lambda j=j: lambda e_: e_.activation(out=dgv[:, j, :], in_=ident_f[:], func=AF.Identity,
                                                                            scale=cw[:, l, c, j:j + 1]))(), [dfr, e_params, e_const])
                S["dg"] = (di, dgv, edg)

            def conv_step(c):
                def f(sl, ev):
                    di, dgv, edg = S["dg"]
                    if c < 3:
                        build_diag(c + 1)
                    last = None
                    for tt in tiles:
                        t0, n = TILES[tt]
                        lc, _ = lcol(tt)
                        ib, b, fb = bk.get()
                        deps = [edg, fb, apad_ev[(c, tt)], S.get("shalo")]
                        if tt < 4:
                            prev = tiles.index(tt) - 1
                            if prev >= 0:
                                deps.append(apad_ev[(c, tiles[prev])])
                        for j in range(CW):
                            if tt < 4:
                                rhs = a_pad[:, c, lc + j:lc + j + n]
                                outv = b[:, 0:n]
                            else:
                                rhs = a_pad_s[:, c, :, j:j + 4]
                                outv = b[:, 0:64].rearrange("p (b t) -> p b t", t=4)
                            ep = P.op("pe", (lambda j=j, rhs=rhs, outv=outv: lambda e_: e_.matmul(outv, lhsT=dgv[:, j, :], rhs=rhs,
                                                                                                  start=(j == 0), stop=(j == CW - 1)))(), deps, inc=(j == CW - 1))
                        ea = P.op("act", lambda e_: e_.activation(out=cact[:, c, lc:lc + n], in_=b[:, 0:n], func=AF.Identity,
                                                                  bias=cb[:, l, c:c + 1]), [ep, S.get("acts_rd"), e_params])
                        bk.release(ib, ea)
                        S[("cy", c, tt)] = ea
                        last = ep
                    dgr.release(di, last)
                    S["conv_rd"] = last
                    return None
                return f

            for c in range(4):
                add_step([], conv_step(c), "conv")

            def conv_ln(sl, ev):
                for tt in tiles:
                    lc, n = lcol(tt)
                    sq_w = [S[("cy", c, tt)] for c in range(4)]
                    e_last = ln_fm(cact, lc, n, clg, clb, l, AF.Silu, sqv_c, ver_c, sq_w)
                    S[("cact", tt)] = e_last
                S["conv_rd_prev"] = S["conv_rd"]
                if h == 0:
                    P.op("dve", lambda e_: e_.tensor_copy(out=ahalo[:], in_=a_pad[:, :, 1024:1054]), [apad_ev[(c_, 1)] for c_ in range(4)])
                if h == 1:
                    ib, b, fb = bk.get()
                    for c in range(4):
                        ep = P.op("pe", (lambda c=c: lambda e_: e_.matmul(b[0:30, c * 128:(c + 1) * 128], lhsT=st_a[:, c, :], rhs=ident_f[:],
                                                                           start=True, stop=True))(), [fb, apad_ev[(3, 3)]], inc=(c == 3))
                    si, so, sfr = stor.get()
                    ea = P.op("act", lambda e_: e_.activation(out=so[0:30, :], in_=b[0:30, :], func=AF.Copy, scale=0.5), [ep, sfr])
                    bk.release(ib, ea)
                    eo_ = P.dma("pool", conv_p[l], so[0:30, :], s_out2[si], [ea])
                    stor.release(si, eo_)
                    ib, b, fb = bk.get()
                    for c in range(4):
                        ep2 = P.op("pe", (lambda c=c: lambda e_: e_.matmul(b[0:64, c * 128:(c + 1) * 128], lhsT=st_as[:, c, :], rhs=ident_f[:],
                                                                            start=True, stop=True))(), [fb, apad_ev[(3, 4)]], inc=(c == 3))
                    si, so2, sfr = stor.get()
                    ea2 = P.op("act", lambda e_: e_.activation(out=so2[0:64, :], in_=b[0:64, :], func=AF.Copy, scale=0.5), [ep2, sfr])
                    bk.release(ib, ea2)
                    for bb in range(NSEQ):
                        eo2 = P.dma("pool", conv_s[l, bb, 26:30, :], so2[bb * 4:(bb + 1) * 4, :], s_out2[si], [ea2])
                    stor.release(si, eo2)
                    S["st_rd"] = [ep, ep2]
                return None

            add_step([], conv_ln)

            op_ = 8704
            pp0, op_ = sv(op_, [1040], F32)
            pp1, op_ = sv(op_, [1040], F32)
            pps, op_ = sv(op_, [4, NSEQ, 19], F32)
            ta, op_ = sv(op_, [528], F32)
            tb, op_ = sv(op_, [528], F32)
            tap, op_ = sv(op_, [528], F32)
            tbp, op_ = sv(op_, [528], F32)
            ta3, op_ = sv(op_, [NSEQ, 20], F32)
            tb3, op_ = sv(op_, [NSEQ, 20], F32)
            mt0, op_ = sv(op_, [512], BF16)
            mt1, op_ = sv(op_, [512], BF16)
            mt2, op_ = sv(op_, [512], BF16)
            st_p, op_ = sv(op_, [4, 16], F32)
            st_ps, op_ = sv(op_, [4, 64], F32)
            pst0, op_ = sv(op_, [512], F32)
            pst1, op_ = sv(op_, [512], F32)
            fix, op_ = sv(op_, [16], F32)
            ppr = Ring([pp0, pp1])
            mtr = Ring([mt0, mt1, mt2])
            pstr = Ring([pst0, pst1])

            def pool_prep(sl, ev):
                P.barrier()
                if h == 0:
                    ed = P.dma("sp", lstage[:], W["pool_w"][l].rearrange("g c d -> c g d"), s_lstage, [S.get("lstage_rd")])
                    S["pw"] = P.op("act", lambda e_: e_.activation(out=pw_b[:], in_=lstage[:], func=AF.Copy), [ed, S.get("pw_rd")])
                    G["pw"] = S["pw"]
                    G["lstage_rd"] = S["pw"]
                if h == 1:
                    src = spool[l].rearrange("b j c -> (b j) c")
                    for r in range(2):
                        si, sv_, sfr = pstr.get()
                        ed = P.dma("sp", sv_[0:120, :], src[r * 120:(r + 1) * 120, :], s_sti2[si], [sfr, S.get("conv_rd")])
                        for bb in range(8):
                            eo_ = P.dma("pool", pool_s[l, 8 * r + bb, 0:11, :], sv_[bb * 15 + 4:bb * 15 + 15, :], s_out2[si], [ed])
                        ib, b, fb = bk.get()
                        for g in range(4):
                            ep = P.op("pe", (lambda g=g, b=b, sv_=sv_: lambda e_: e_.matmul(b[:, g * 120:(g + 1) * 120], lhsT=sv_[0:120, g * 128:(g + 1) * 128],
                                                                               rhs=ident_f[0:120, 0:120], start=True, stop=True))(), [ed, fb], inc=(g == 3))
                        for g in range(4):
                            ea = P.op("act", (lambda g=g, b=b, r=r: lambda e_: e_.activation(
                                out=pps[:, g, 8 * r:8 * r + 8, 0:15], in_=b[:, g * 120:(g + 1) * 120].rearrange("p (b j) -> p b j", b=8),
                                func=AF.Copy))(), [ep])
                        bk.release(ib, ea)
                        pstr.release(si, [ep, eo_])
                        S["pps"] = ea
                return None

            add_step([], pool_prep)

            def pool_step(gp):
                def f(sl, ev):
                    wv = sl[0][:, 0:2048].rearrange("p (k c) -> p k c", k=8)
                    last = None
                    pend = []
                    for gg in range(2):
                        g = gp * 2 + gg
                        w = WIN[g]
                        nl = {2: 1, 4: 2, 8: 3, 16: 4}[w]
                        pi, pp, pfr = ppr.get()
                        if h == 0:
                            e_h = P.op("dve", lambda e_: e_.memset(pp[:, 0:15], 0.0), [pfr, S.get("conv_rd")])
                        else:
                            e_h = P.op("dve", lambda e_: e_.tensor_copy(out=pp[:, 0:15], in_=phalo[:, g, 0:15]), [pfr, S.get("conv_rd")])
                        for tt in tiles:
                            t0, n = TILES[tt]
                            lc, _ = lcol(tt)
                            ib, b, fb = bk.get()
                            for k in range(8):
                                ep = P.op("pe", (lambda k=k: lambda e_: e_.matmul(b[:, 0:n], lhsT=wv[:, k, gg * 128:(gg + 1) * 128], rhs=hT[:, k, t0:t0 + n],
                                                                                   start=(k == 0), stop=(k == 7)))(), [ev[0], h_ev[tt], fb], inc=(k == 7))
                            h_rd[tt] = ep
                            last = ep
                            if len(pend) >= 2:
                                pend.pop(0)()
                            if tt < 4:
                                ea = P.op("act", lambda e_: e_.activation(out=pp[:, 15 + lc:15 + lc + n], in_=b[:, 0:n], func=AF.Copy), [ep, pfr, e_h])
                                bk.release(ib, ea)
                                src = pp[:, lc:lc + 15 + n]
                                L = n + 15
                                cur = src
                                weng = "pool" if (g % 2 == 0) else "dve"
                                bufs = [tap, tbp] if weng == "pool" else [ta, tb]
                                e_prev = [ea, e_h, S.get("win_rd_p" if weng == "pool" else "win_rd")]
                                sh = 1
                                lo = 0
                                for lv in range(nl):
                                    dst = bufs[lv % 2]
                                    lo = lo + sh
                                    e_prev = P.op(weng, (lambda dst=dst, cur=cur, lo=lo, sh=sh, L=L: lambda e_: e_.tensor_tensor(
                                        out=dst[:, lo:L], in0=cur[:, lo:L], in1=cur[:, lo - sh:L - sh], op=ALU.add))(), e_prev)
                                    cur = dst
                                    sh *= 2
                                mi, mv_, mfr = mtr.get()
                                em = P.op("dve", lambda e_: e_.scalar_tensor_tensor(out=mv_[:, 0:n], in0=cur[:, 15:15 + n], scalar=1.0 / w,
                                                                                    in1=src[:, 15:15 + n], op0=ALU.mult, op1=ALU.subtract), [e_prev, mfr])
                                if h == 0 and tt == 0:
                                    ef = P.op("dve", lambda e_: e_.tensor_tensor(out=fix[:, 0:w - 1], in0=cur[:, 15:15 + w - 1], in1=rc[:, 0:w - 1], op=ALU.mult), [em, e_const])
                                    em = P.op("dve", lambda e_: e_.tensor_tensor(out=mv_[:, 0:w - 1], in0=fix[:, 0:w - 1], in1=src[:, 15:15 + w - 1], op=ALU.subtract), [ef])
                                S["win_rd"] = em
                                if weng == "pool":
                                    S["win_rd_p"] = em
                            else:
                                ea = P.op("act", lambda e_: e_.activation(out=pps[:, g, :, 15:19], in_=b[:, 0:64].rearrange("p (b t) -> p b t", t=4),
                                                                          func=AF.Copy), [ep, S.get("pps")])
                                bk.release(ib, ea)
                                cur = pps[:, g]
                                bufs = [ta3, tb3]
                                e_prev = [ea, S.get("win_rd")]
                                sh = 1
                                lo = 0
                                for lv in range(nl):
                                    dst = bufs[lv % 2]
                                    lo = lo + sh
                                    e_prev = P.op("dve", (lambda dst=dst, cur=cur, lo=lo, sh=sh: lambda e_: e_.tensor_tensor(
                                        out=dst[:, :, lo:19], in0=cur[:, :, lo:19], in1=cur[:, :, lo - sh:19 - sh], op=ALU.add))(), e_prev)
                                    cur = dst
                                    sh *= 2
                                mi, mv_, mfr = mtr.get()
                                em = P.op("dve", lambda e_: e_.scalar_tensor_tensor(out=mv_[:, 0:64].rearrange("p (b t) -> p b t", t=4), in0=cur[:, :, 15:19],
                                                                                    scalar=1.0 / w, in1=pps[:, g, :, 15:19], op0=ALU.mult, op1=ALU.subtract), [e_prev, mfr])
                                S["win_rd"] = em
                                est = P.op("dve", lambda e_: e_.tensor_copy(out=st_ps[:, g, :].rearrange("p (b t) -> p b t", t=4), in_=pps[:, g, :, 15:19]), [ea, S.get("stp_rd")])
                                S["stps"] = est
                            def fin(g=g, tt=tt, n=n, lc=lc, mi=mi, mv_=mv_, em=em):
                                i2, b2, f2 = bk.get()
                                ep2 = P.op("pe", lambda e_: e_.matmul(b2[:, 0:n], lhsT=pw_b[:, g, :], rhs=mv_[:, 0:n], start=True, stop=True), [em, f2, G["pw"]])
                                mtr.release(mi, ep2)
                                G["pw_rd"] = ep2
                                ea2 = P.op("act", lambda e_: e_.activation(out=pact[:, g, lc:lc + n], in_=b2[:, 0:n], func=AF.Copy, scale=psc[:, l, g:g + 1]),
                                           [ep2, S.get("acts_rd"), e_params])
                                bk.release(i2, ea2)
                                S[("pact", tt)] = ea2
                            pend.append(fin)
                        if h == 0:
                            eh = P.op("dve", lambda e_: e_.tensor_copy(out=phalo[:, g, 0:15], in_=pp[:, 1024:1039]), [S["win_rd"]])
                            ppr.release(pi, [eh])
                        else:
                            eh = P.op("dve", lambda e_: e_.tensor_copy(out=st_p[:, g, 0:15], in_=pp[:, 1024:1039]), [S["win_rd"], S.get("stp_rd")])
                            S["stp"] = eh
                            ppr.release(pi, [eh])
                    while pend:
                        pend.pop(0)()
                    return last
                return f

            for gp in range(2):
                add_step([[(0, 8, 256, win[:, 1024 + gp * 256:1024 + (gp + 1) * 256])]], pool_step(gp), "pool")

            def pool_states(sl, ev):
                if h == 1:
                    ib, b, fb = bk.get()
                    for g in range(4):
                        ep = P.op("pe", (lambda g=g: lambda e_: e_.matmul(b[0:15, g * 128:(g + 1) * 128], lhsT=st_p[:, g, 0:15], rhs=ident_f[:],
                                                                           start=True, stop=True))(), [fb, S["stp"]], inc=(g == 3))
                    si, so, sfr = pstr.get()
                    ea = P.op("act", lambda e_: e_.activation(out=so[0:15, :], in_=b[0:15, :], func=AF.Copy), [ep, sfr])
                    bk.release(ib, ea)
                    eo_ = P.dma("pool", pool_p[l], so[0:15, :], s_out2[si], [ea])
                    pstr.release(si, eo_)
                    ib, b, fb = bk.get()
                    for g in range(4):
                        ep2 = P.op("pe", (lambda g=g: lambda e_: e_.matmul(b[0:64, g * 128:(g + 1) * 128], lhsT=st_ps[:, g, :], rhs=ident_f[:],
                                                                            start=True, stop=True))(), [fb, S["stps"]], inc=(g == 3))
                    si, so2, sfr = pstr.get()
                    ea2 = P.op("act", lambda e_: e_.activation(out=so2[0:64, :], in_=b[0:64, :], func=AF.Copy), [ep2, sfr])
                    bk.release(ib, ea2)
                    for bb in range(NSEQ):
                        eo2 = P.dma("pool", pool_s[l, bb, 11:15, :], so2[bb * 4:(bb + 1) * 4, :], s_out2[si], [ea2])
                    pstr.release(si, eo2)
                    S["stp_rd"] = [ep, ep2]
                S["pool_done"] = S["win_rd"]
                return None

            add_step([], pool_states)

            os_ = o_acts
            vn_fm2, os_ = sv(os_, [4, 528], BF16)
            vn_fm, os_ = sv(os_, [4, 528], BF16)
            vn_tok, os_ = sv(os_, [9, 512], BF16)
            sqv_s, os_ = sv(os_, [4, 512], BF16)
            ve1, os_ = sv(os_, [512], F32)
            ve1b, os_ = sv(os_, [512], F32)
            us0, os_ = sv(os_, [512], F32)
            us1, os_ = sv(os_, [512], F32)
            vn_f32, os_ = sv(os_, [4, 64], F32)
            vns, os_ = sv(os_, [512], F32)
            usr = Ring([us0, us1])
            ver_s = Ring([(ve1, ve1b, us0)])

            def sgu_prep(sl, ev):
                P.barrier()
                if h == 0:
                    ed = P.dma("sp", lstage[:], W["sgu_w"][l].rearrange("g t s -> t g s"), s_lstage, [G.get("lstage_rd")])
                    e1 = P.op("pool", lambda e_: e_.affine_select(out=lstage[:], in_=lstage[:], pattern=[[0, 4], [-1, 128]], compare_op=ALU.is_ge,
                                                                  fill=0.0, base=0, channel_multiplier=1), [ed])
                    ib, b, fb = bk.get()
                    for g in range(4):
                        ep = P.op("pe", (lambda g=g: lambda e_: e_.matmul(b[:, g * 128:(g + 1) * 128], lhsT=lstage[:, g, :], rhs=ident_f[:],
                                                                           start=True, stop=True))(), [e1, fb], inc=(g == 3))
                    ea = P.op("act", lambda e_: e_.activation(out=wct[:], in_=b[:, :].rearrange("p (g t) -> p g t", g=4), func=AF.Copy), [ep, G.get("wct_rd")])
                    bk.release(ib, ea)
                    G["wct"] = ea
                    G["lstage_rd"] = ep
                    e0 = P.op("pool", lambda e_: e_.memset(bds[:], 0.0), [G.get("bds_rd")])
                    src = W["sgu_w"][l][:, 0:4, 0:4].rearrange("g t s -> t g s")
                    for bb in range(NSEQ):
                        edb = P.dma("sp", bds[bb * 4:(bb + 1) * 4, :, bb * 4:(bb + 1) * 4], src, s_bds, [e0], allow_slow_non_contiguous=True)
                    e2 = P.op("pool", lambda e_: e_.affine_select(out=bds[:], in_=bds[:], pattern=[[0, 4], [-1, 64]], compare_op=ALU.is_ge,
                                                                  fill=0.0, base=0, channel_multiplier=1), [edb])
                    ib, b, fb = bk.get()
                    for g in range(4):
                        ep = P.op("pe", (lambda g=g: lambda e_: e_.matmul(b[0:64, g * 64:(g + 1) * 64], lhsT=bds[:, g, :], rhs=ident_f[0:64, 0:64],
                                                                           start=True, stop=True))(), [e2, fb], inc=(g == 3))
                    e3 = P.op("act", lambda e_: e_.activation(out=bd_b[:], in_=b[0:64, 0:256].rearrange("p (g t) -> p g t", g=4), func=AF.Copy), [ep, G.get("bd_rd")])
                    bk.release(ib, e3)
                    G["bd"] = e3
                    G["bds_rd"] = ep
                    eb = P.dma("sp", bs[0:1, :, 0:128], W["sgu_b"][l:l + 1], s_bs, [G.get("bs_rd")])
                    eb2 = P.op("dve", lambda e_: e_.tensor_copy(out=bs[0:1, :, 128:132], in_=bs[0:1, :, 0:4]), [eb])
                    wd_ = 4
                    while wd_ < 64:
                        eb2 = P.op("dve", (lambda wd_=wd_: lambda e_: e_.tensor_copy(out=bs[0:1, :, 128 + wd_:128 + 2 * wd_], in_=bs[0:1, :, 128:128 + wd_]))(), [eb2])
                        wd_ *= 2
                    G["bs"] = eb2
                return None

            add_step([], sgu_prep)

            def v_step(sl, ev):
                wv = [sl[i][:, 0:2048].rearrange("p (k c) -> p k c", k=8) for i in range(2)]
                last = [None]
                vbufs = [vn_fm, vn_fm2]
                vrd = [None, None]

                def emit_tr(idx, tt, vb, e_ln):
                    t0, n = TILES[tt]
                    lc, _ = lcol(tt)
                    nsub = (n + 127) // 128
                    ep = None
                    for i in range(nsub):
                        np_ = min(128, n - i * 128)
                        li = (lc // 128 + i) if tt < 4 else 8
                        ib, b, fb = bk.get()
                        for c in range(4):
                            ep = P.op("pe", (lambda c=c, i=i, np_=np_, b=b: lambda e_: e_.matmul(b[0:np_, c * 128:(c + 1) * 128], lhsT=vb[:, c, i * 128:i * 128 + np_],
                                                                                            rhs=ident_b[:], start=True, stop=True))(), [e_ln, fb, e_const], inc=(c == 3))
                        ea = P.op("act", (lambda li=li, np_=np_, b=b: lambda e_: e_.activation(out=vn_tok[0:np_, li, :], in_=b[0:np_, :], func=AF.Copy))(),
                                  [ep, S.get("vntok_rd")])
                        bk.release(ib, ea)
                        S[("vntok", li)] = ea
                    vrd[idx % 2] = ep
                    S["vnfm_rd"] = ep
                    last[0] = ep
                    if tt == 4:
                        ib, b, fb = bk.get()
                        for c in range(4):
                            ep = P.op("pe", (lambda c=c, b=b: lambda e_: e_.matmul(b[0:64, c * 128:(c + 1) * 128], lhsT=vn_f32[:, c, :], rhs=ident_f[:],
                                                                              start=True, stop=True))(), [e_ln, fb], inc=(c == 3))
                        ea = P.op("act", lambda e_: e_.activation(out=vns[0:64, :], in_=b[0:64, :], func=AF.Copy), [ep, S.get("vns_rd")])
                        bk.release(ib, ea)
                        eo_ = P.dma("pool", cv_s[l], vns[0:64, :], s_outv, [ea])
                        G["vns_rd"] = eo_
                        S["vnf32_rd"] = ep
                        last[0] = ep

                pend = None
                for idx, tt in enumerate(tiles):
                    t0, n = TILES[tt]
                    vb = vbufs[idx % 2]
                    evs = []
                    for c in range(4):
                        ib, b, fb = bk.get()
                        for k in range(8):
                            ep = P.op("pe", (lambda k=k, c=c, b=b: lambda e_: e_.matmul(b[:, 0:n], lhsT=wv[c // 2][:, k, (c % 2) * 128:(c % 2 + 1) * 128],
                                                                                   rhs=hT[:, k, t0:t0 + n], start=(k == 0), stop=(k == 7)))(),
                                      [ev[0], ev[1], h_ev[tt], fb], inc=(k == 7))
                        ea = P.op("act", (lambda c=c, b=b: lambda e_: e_.activation(out=vb[:, c, 0:n], in_=b[:, 0:n], func=AF.Copy))(),
                                  [ep, vrd[idx % 2], S.get("pool_done")])
                        bk.release(ib, ea)
                        evs.append(ea)
                        last[0] = ep
                    h_rd[tt] = ep
                    extra = vn_f32 if tt == 4 else None
                    e_ln = ln_fm(vb, 0, n, slg, slb, l, AF.Identity, sqv_s, ver_s, evs, extra_out=extra)
                    S["ln_last"] = e_ln
                    if pend is not None:
                        emit_tr(*pend)
                    pend = (idx, tt, vb, e_ln)
                emit_tr(*pend)
                return last[0]

            add_step([[(0, 8, 256, win[:, 2048:2304])], [(0, 8, 256, win[:, 2304:2560])]], v_step)

            def spatial_step(gp):
                def f(sl, ev):
                    wv = sl[0][:, 0:2048].rearrange("p (k c) -> p k c", k=8)
                    last = None
                    for gg in range(2):
                        g = gp * 2 + gg
                        for tt in tiles:
                            t0, n = TILES[tt]
                            lc, _ = lcol(tt)
                            iu, bu, fu = bk.get()
                            im, bm, fm = bk.get()
                            for k in range(8):
                                epu = P.op("pe", (lambda k=k: lambda e_: e_.matmul(bu[:, 0:n], lhsT=wv[:, k, gg * 128:(gg + 1) * 128], rhs=hT[:, k, t0:t0 + n],
                                                                                    start=(k == 0), stop=(k == 7)))(), [ev[0], h_ev[tt], fu], inc=(k == 7))
                            h_rd[tt] = epu
                            if tt < 4:
                                for i in range(4):
                                    li = lc // 128 + i
                                    P.op("pe", (lambda i=i: lambda e_: e_.matmul(bm[:, i * 128:(i + 1) * 128], lhsT=ones_f[0:1, :], rhs=bs[0:1, g, 0:128],
                                                                                  start=True, stop=False))(), [fm, G["bs"], e_const], inc=False)
                                    epm = P.op("pe", (lambda i=i, li=li: lambda e_: e_.matmul(bm[:, i * 128:(i + 1) * 128], lhsT=vn_tok[:, li, g * 128:(g + 1) * 128],
                                                                                           rhs=wct[:, g, :], start=False, stop=True))(), [S[("vntok", li)], G["wct"]], inc=(i == 3))
                            else:
                                P.op("pe", lambda e_: e_.matmul(bm[:, 0:64], lhsT=ones_f[0:1, :], rhs=bs[0:1, g, 128:192], start=True, stop=False),
                                     [fm, G["bs"], e_const], inc=False)
                                epm = P.op("pe", lambda e_: e_.matmul(bm[:, 0:64], lhsT=vn_tok[0:64, 8, g * 128:(g + 1) * 128], rhs=bd_b[:, g, :],
                                                                      start=False, stop=True), [S[("vntok", 8)], G["bd"]])
                            G["wct_rd"] = epm
                            G["bd_rd"] = epm
                            G["bs_rd"] = epm
                            S["vntok_rd"] = epm
                            ui, uv, ufr = usr.get()
                            ea = P.op("act", lambda e_: e_.activation(out=uv[:, 0:n], in_=bu[:, 0:n], func=AF.Copy), [epu, ufr, S.get("ln_last")])
                            bk.release(iu, ea)
                            ed = P.op("dve", lambda e_: e_.tensor_tensor(out=sact[:, g, lc:lc + n], in0=bm[:, 0:n], in1=uv[:, 0:n], op=ALU.mult),
                                      [ea, epm, S.get("acts_rd")])
                            bk.release(im, ed)
                            usr.release(ui, ed)
                            S[("sact", tt)] = ed
                            last = epm
                    return last
                return f

            for gp in range(2):
                add_step([[(0, 8, 256, win[:, 1536 + gp * 256:1536 + (gp + 1) * 256])]], spatial_step(gp), "spatial")

            om = o_acts
            mrg, om = sv(om, [8, 1088], BF16)
            acc, om = sv(om, [3, 512], F32)
            th2, om = sv(om, [512], F32)
            th3, om = sv(om, [512], F32)
            thr2 = Ring([th2, th3])
            wouts = [W["w_conv_out"][l], W["w_pool_out"][l], W["w_sgu_out"][l]]
            act_ready = ["cact", "pact", "sact"]

            def merge_step(m, br):
                def f(sl, ev):
                    gv = sl[0][:, 0:1024].rearrange("p (k c) -> p k c", k=8)
                    ov = sl[0][:, 1024:1536].rearrange("p (k c) -> p k c", k=4)
                    last = None
                    for ti, tt in enumerate(tiles):
                        t0, n = TILES[tt]
                        lc, _ = lcol(tt)
                        ig, bg, fg = bk.get()
                        iy, by, fy = bk.get()
                        for k in range(8):
                            epg = P.op("pe", (lambda k=k: lambda e_: e_.matmul(bg[:, 0:n], lhsT=gv[:, k, :], rhs=hT[:, k, t0:t0 + n],
                                                                                start=(k == 0), stop=(k == 7)))(), [ev[0], h_ev[tt], fg], inc=(k == 7))
                        h_rd[tt] = epg
                        for k in range(4):
                            epy = P.op("pe", (lambda k=k: lambda e_: e_.matmul(by[:, 0:n], lhsT=ov[:, k, :], rhs=acts[br][:, k, lc:lc + n],
                                                                                start=(k == 0), stop=(k == 3)))(), [fy, S[(act_ready[br], tt)]], inc=(k == 3))
                        S["acts_rd"] = epy
                        ti_, thv, tfr = thr2.get()
                        ea = P.op("act", lambda e_: e_.activation(out=thv[:, 0:n], in_=bg[:, 0:n], func=AF.Tanh, scale=0.5), [epg, tfr, S.get("sgu_tmp_rd")])
                        bk.release(ig, ea)
                        if br == 0:
                            ed = P.op("dve", lambda e_: e_.scalar_tensor_tensor(out=acc[:, ti, 0:n], in0=thv[:, 0:n], scalar=1.0, in1=by[:, 0:n],
                                                                                op0=ALU.add, op1=ALU.mult), [ea, epy, S.get("sgu_tmp_rd")])
                        else:
                            e1 = P.op("dve", lambda e_: e_.scalar_tensor_tensor(out=by[:, 0:n], in0=thv[:, 0:n], scalar=1.0, in1=by[:, 0:n],
                                                                                op0=ALU.add, op1=ALU.mult), [ea, epy])
                            if br == 1:
                                ed = P.op("dve", lambda e_: e_.tensor_tensor(out=acc[:, ti, 0:n], in0=acc[:, ti, 0:n], in1=by[:, 0:n], op=ALU.add), [e1])
                            else:
                                ed = P.op("dve", lambda e_: e_.tensor_tensor(out=mrg[:, m, lc:lc + n], in0=acc[:, ti, 0:n], in1=by[:, 0:n], op=ALU.add),
                                          [e1, S.get("mrg_rd")])
                                S[("mrg", tt)] = ed
                        bk.release(iy, ed)
                        thr2.release(ti_, ed)
                        last = epy
                    return last
                return f

            def mark_sgu_done(sl, ev):
                P.barrier()
                if h == 0:
                    rmsnorm(gidx, HALVES[1], 26880)
                else:
                    rmsnorm(l * 3 + 2, HALVES[0], 26880)
                S["sgu_tmp_rd"] = [S.get("vnfm_rd"), S.get("vntok_rd"), S.get("vnf32_rd"), G.get("vns_rd")]
                S["mrg_rd"] = S["sgu_tmp_rd"]
                return None

            add_step([], mark_sgu_done)
            for m in range(8):
                for br in range(3):
                    add_step([[(0, 8, 128, win[:, 2560 + br * 1024 + m * 128:2560 + br * 1024 + (m + 1) * 128]),
                               (1024, 4, 128, wouts[br][:, m * 128:(m + 1) * 128])]], merge_step(m, br), "merge")

            def wo_step(np2):
                def f(sl, ev):
                    wv = sl[0][:, 0:2048].rearrange("p (k c) -> p k c", k=8)
                    last = None
                    for nn in range(2):
                        nch = np2 * 2 + nn
                        for tt in tiles:
                            t0, n = TILES[tt]
                            lc, _ = lcol(tt)
                            ib, b, fb = bk.get()
                            for k in range(8):
                                ep = P.op("pe", (lambda k=k: lambda e_: e_.matmul(b[:, 0:n], lhsT=wv[:, k, nn * 128:(nn + 1) * 128], rhs=mrg[:, k, lc:lc + n],
                                                                                   start=(k == 0), stop=(k == 7)))(), [ev[0], S[("mrg", tt)], fb], inc=(k == 7))
                            ed = P.op("dve", lambda e_: e_.scalar_tensor_tensor(out=xT[:, nch, t0:t0 + n], in0=b[:, 0:n], scalar=0.5, in1=xT[:, nch, t0:t0 + n],
                                                                                op0=ALU.mult, op1=ALU.add), [ep])
                            bk.release(ib, ed)
                            x_ev[tt] = ed
                            last = ep
                    S["mrg_rd"] = last
                    G["scr_rd"] = last
                    return last
                return f

            for np2 in range(4):
                add_step([[(0, 8, 256, W["w_o"][l][:, np2 * 256:(np2 + 1) * 256])]], wo_step(np2), "wo")

        G = {}

        def final_out(sl, ev):
            P.barrier()
            o = 0
            of, o = sv(o, [8, 512], F32)
            yt0, o = sv(o, [1024], F32)
            yt1, o = sv(o, [1024], F32)
            sqv, o = sv(o, [4, 512], BF16)
            r0, o = sv(o, [512], F32)
            r1, o = sv(o, [512], F32)
            rsr = Ring([(r0, r1)])
            ytr = Ring([yt0, yt1])
            of_rd = None
            for tt in range(5):
                t0, n = TILES[tt]
                rs, ri, eo = rms_stats(tt, sqv, rsr)
                for k in range(8):
                    ed = P.op("dve", (lambda k=k: lambda e_: e_.scalar_tensor_tensor(
                        out=of[:, k, 0:n], in0=xT[:, k, t0:t0 + n], scalar=gn[:, 12, k:k + 1], in1=rs[:, 0:n], op0=ALU.mult, op1=ALU.mult))(),
                        [eo, of_rd, G.get("scr_rd")])
                rsr.release(ri, ed)
                nsub = (n + 127) // 128
                for i in range(nsub):
                    np_ = min(128, n - i * 128)
                    yi, yt, yfr = ytr.get()
                    for hf in range(2):
                        ib, b, fb = bk.get()
                        for kk in range(4):
                            k = hf * 4 + kk
                            ep = P.op("pe", (lambda kk=kk, k=k, i=i, np_=np_, b=b: lambda e_: e_.matmul(b[0:np_, kk * 128:(kk + 1) * 128], lhsT=of[:, k, i * 128:i * 128 + np_],
                                                                                                   rhs=ident_f[:], start=True, stop=True))(), [ed, fb], inc=(kk == 3))
                        ea = P.op("act", (lambda hf=hf, np_=np_, b=b, yt=yt: lambda e_: e_.activation(out=yt[0:np_, hf * 512:(hf + 1) * 512], in_=b[0:np_, :], func=AF.Copy))(),
                                  [ep, yfr])
                        bk.release(ib, ea)
                    of_rd = ep
                    eo_ = P.dma("pool", y_out[t0 + i * 128:t0 + i * 128 + np_, :], yt[0:np_, :], s_out2[yi], [ea])
                    ytr.release(yi, eo_)
            return None

        load_x()
        _par = P.q["sp"][_sp_n0:_sp_n1]
        del P.q["sp"][_sp_n0:_sp_n1]
        P.q["sp"].extend(_par)
        for l in range(depth):
            ffn(l, 1)
            for h in range(2):
                mixer(l, h)
            ffn(l, 2)
        if limit is not None:
            del steps[limit:]
        add_step([], final_out)

        slab_list = []
        for si, (slabs, comp) in enumerate(steps):
            for parts in slabs:
                slab_list.append((si, parts))
        stage_free = [None] * NS
        bf_free = [None] * NB
        slab_ready = {}
        issued = [0]

        def issue(kidx):
            si, parts = slab_list[kidx]
            ss = kidx % NS
            bs_ = kidx % NB
            tot = 0
            ed = None
            for (off, nk, ncol, src) in parts:
                dst = stg[:, ss, off:off + nk * ncol].rearrange("p (k c) -> p k c", k=nk)
                ed = P.dma("sp", dst, src.rearrange("(k p) c -> p k c", p=128), s_stage[ss], [stage_free[ss]], nobar=True)
                tot = max(tot, off + nk * ncol)
            ec = P.op("pool", lambda e_: e_.tensor_copy(out=wbf[:, bs_, 0:tot], in_=stg[:, ss, 0:tot]), [ed, bf_free[bs_]], nobar=True)
            stage_free[ss] = ec
            slab_ready[kidx] = ec

        first_slab_of_step = {}
        kk_ = 0
        for si, (slabs, comp) in enumerate(steps):
            first_slab_of_step[si] = kk_
            kk_ += len(slabs)
        nslab = len(slab_list)
        for si, (slabs, comp) in enumerate(steps):
            k0 = first_slab_of_step[si]
            want = min(nslab, k0 + NB)
            while issued[0] < want:
                issue(issued[0])
                issued[0] += 1
            sl = [wbf[:, (k0 + i) % NB, :] for i in range(len(slabs))]
            evs = [slab_ready[k0 + i] for i in range(len(slabs))]
            marks.append((getattr(comp, "__name__", "step"), len(P.q["pe"])))
            r = comp(sl, evs)
            for i in range(len(slabs)):
                assert r is not None, si
                bf_free[(k0 + i) % NB] = r

        for so_ in s_out2 + [s_outv]:
            P.op("pool", (lambda so_=so_: lambda e_: e_.wait_ge(so_[0], so_[1]))(), inc=False, nobar=True)
        if os.environ.get("DUMP_MARKS"):
            import json as _json
            _json.dump(marks, open(os.environ["DUMP_MARKS"], "w"))
        P.emit(block)
    return nc


_CACHE = {}


def _get_program(depth=DEPTH):
    if depth not in _CACHE:
        _CACHE[depth] = build_program(depth)
    return _CACHE[depth]


WEIGHT_NAMES = ["ffn1_norm", "ffn1_w_gate_up", "ffn1_w_down", "mix_norm", "w_in", "conv_dw_w", "conv_dw_b", "conv_ln_g",
                "conv_ln_b", "w_conv_out", "pool_w", "pool_scale", "w_pool_out", "sgu_ln_g", "sgu_ln_b", "sgu_w", "sgu_b",
                "w_sgu_out", "w_o", "ffn2_norm", "ffn2_w_gate_up", "ffn2_w_down", "final_norm"]


def kernel(**inputs):
    nc = _get_program(DEPTH)
    f32 = lambda a: np.ascontiguousarray(np.asarray(a, dtype=np.float32))
    xp = f32(inputs["x_prompt"])
    xs = f32(inputs["x_sample"])
    sc = f32(inputs["state_conv"])
    sp = f32(inputs["state_pool"])
    wts = {k: f32(inputs[k]) for k in WEIGHT_NAMES}
    in_maps = []
    for i in range(NCORES):
        m = dict(wts)
        m["xin"] = np.concatenate([xp[i], xs[NSEQ * i:NSEQ * (i + 1)].reshape(TS, D)], axis=0)
        m["sconv"] = np.ascontiguousarray(sc[:, NSEQ * i:NSEQ * (i + 1)])
        m["spool"] = np.ascontiguousarray(sp[:, NSEQ * i:NSEQ * (i + 1)])
        in_maps.append(m)
    res = run_bass_kernel_spmd(nc, in_maps, core_ids=list(range(NCORES)))
    R = res.results
    y_prompt = np.stack([R[i]["y"][:TP] for i in range(NCORES)], axis=0)
    y_sample = np.concatenate([R[i]["y"][TP:].reshape(NSEQ, 4, D) for i in range(NCORES)], axis=0)
    conv_prompt = np.stack([R[i]["conv_p"] for i in range(NCORES)], axis=1)
    conv_sample = np.concatenate([R[i]["conv_s"] for i in range(NCORES)], axis=1)
    pool_prompt = np.stack([R[i]["pool_p"] for i in range(NCORES)], axis=1)
    pool_sample = np.concatenate([R[i]["pool_s"] for i in range(NCORES)], axis=1)
    chunk_v = np.concatenate([R[i]["cv_s"].reshape(DEPTH, NSEQ, 4, 512) for i in range(NCORES)], axis=1)
    return (y_prompt, y_sample, conv_prompt, conv_sample, pool_prompt, pool_sample, chunk_v)
```
